# Optimizing a Trainium2 kernel written in Bass

```python
import jax, jax.numpy as jnp
from jax import lax
import numpy as np

D_MODEL = 1024
BATCH = 16
SEQ = 256
DEPTH = 4
DEC_BATCH = 4
DEC_SEQ = 1024
PAST_LEN = 256

GRID_W = 64
HEAD_DIM = 64
WIDTH_A = D_MODEL // 2
N_HEADS_A = WIDTH_A // HEAD_DIM
WIDTH_B = D_MODEL // 4
N_GROUPS_B = 4
GROUP_B = WIDTH_B // N_GROUPS_B
WIDTH_C = D_MODEL // 4
N_GROUPS_C = 4
GROUP_C = WIDTH_C // N_GROUPS_C
CHUNK = 128
MIX_WIDTH = WIDTH_A + WIDTH_B + WIDTH_C
PROJ_WIDTH = 3 * WIDTH_A + WIDTH_B + 2 * WIDTH_C
WIN_ROWS_MAX = 8
WIN_COLS = 16
D_FF = ((8 * D_MODEL // 3 + 127) // 128) * 128
N_SUB = 3
EPS = 1e-6
NEG = -1e30

kernel_name = 'hybrid_natten_fnet_gmlp_diffusion_step'


def rmsnorm(x, g):
    xf = x.astype(jnp.float32)
    y = xf * lax.rsqrt(jnp.mean(xf * xf, axis=-1, keepdims=True) + EPS)
    return (y * g.astype(jnp.float32)).astype(x.dtype)


def modulate(x, g, shift, scale):
    return rmsnorm(x, g) * (1 + scale) + shift


def swiglu(h, w_in, w_out):
    gate, up = jnp.split(h @ w_in, 2, axis=-1)
    return (jax.nn.silu(gate) * up) @ w_out


def context_attention(q, k, v):
    B, S, H, Dh = q.shape
    s = jnp.einsum('bqhd,bkhd->bhqk', q, k).astype(jnp.float32) * (Dh ** -0.5)
    p = jax.nn.softmax(s, axis=-1).astype(v.dtype)
    return jnp.einsum('bhqk,bkhd->bqhd', p, v).reshape(B, S, H * Dh)


def neighbourhood_attention(q, k, v, ck, cv, rpb):
    B, N, H, Dh = q.shape
    rows = N // GRID_W
    wr = min(WIN_ROWS_MAX, rows)
    r = np.arange(rows)
    row_start = np.clip(r - wr // 2, 0, rows - wr)
    row_idx = row_start[:, None] + np.arange(wr)[None, :]
    j = np.arange(GRID_W)
    col_start = np.clip(j - WIN_COLS // 2, 0, GRID_W - WIN_COLS)
    col_mask = (j[None, :] >= col_start[:, None]) & (j[None, :] < col_start[:, None] + WIN_COLS)
    row_off = row_idx - r[:, None] + WIN_ROWS_MAX - 1
    col_off = np.clip(j[None, :] - j[:, None] + WIN_COLS - 1, 0, 2 * WIN_COLS - 2)
    bias = rpb[:, row_off[:, None, :, None], col_off[None, :, None, :]]

    qg = q.reshape(B, rows, GRID_W, H, Dh)
    kg = k.reshape(B, rows, GRID_W, H, Dh)[:, row_idx]
    vg = v.reshape(B, rows, GRID_W, H, Dh)[:, row_idx]
    scale = Dh ** -0.5
    s_loc = jnp.einsum('brqhd,brakhd->bhrqak', qg, kg).astype(jnp.float32) * scale + bias.astype(jnp.float32)[None]
    s_loc = jnp.where(col_mask[None, None, None, :, None, :], s_loc, NEG)
    s_ctx = jnp.einsum('brqhd,bchd->bhrqc', qg, ck).astype(jnp.float32) * scale
    L = wr * GRID_W
    s = jnp.concatenate([s_loc.reshape(B, H, rows, GRID_W, L), s_ctx], axis=-1)
    p = jax.nn.softmax(s, axis=-1).astype(v.dtype)
    p_loc = p[..., :L].reshape(B, H, rows, GRID_W, wr, GRID_W)
    p_ctx = p[..., L:]
    o = jnp.einsum('bhrqak,brakhd->brqhd', p_loc, vg) + jnp.einsum('bhrqc,bchd->brqhd', p_ctx, cv)
    return o.reshape(B, N, H * Dh)


def fourier_mix(f):
    B, N, _ = f.shape
    fg = f.reshape(B, N, N_GROUPS_B, GROUP_B).astype(jnp.float32)
    out = jnp.fft.fftn(fg, axes=(1, 3), norm='ortho').real
    return out.reshape(B, N, WIDTH_B).astype(f.dtype)


def gmlp_mix(z, gn, ws, gb):
    B, N, _ = z.shape
    u, v = jnp.split(jax.nn.gelu(z), 2, axis=-1)
    shp = (B, N // CHUNK, CHUNK, N_GROUPS_C, GROUP_C)
    u = u.reshape(shp)
    vf = v.reshape(shp).astype(jnp.float32)
    mu = jnp.mean(vf, axis=-1, keepdims=True)
    var = jnp.mean(jnp.square(vf - mu), axis=-1, keepdims=True)
    vn = ((vf - mu) * lax.rsqrt(var + EPS) * gn.astype(jnp.float32)).astype(z.dtype)
    sp = jnp.einsum('gpq,bnqgc->bnpgc', ws, vn) + gb.T[:, :, None]
    return (u * sp).reshape(B, N, WIDTH_C)


def trunk_layer(x, mod, npre, npost, fw_in, fw_out, w_in, w_out, gn, ws, gb, attn_fn):
    B, N, _ = x.shape
    h = modulate(x, npre[0], mod[:, :, 0], mod[:, :, 1])
    x = x + 0.5 * mod[:, :, 2] * rmsnorm(swiglu(h, fw_in[0], fw_out[0]), npost[0])

    h = modulate(x, npre[1], mod[:, :, 3], mod[:, :, 4])
    z = h @ w_in
    q, k, v, f, g = jnp.split(z, [WIDTH_A, 2 * WIDTH_A, 3 * WIDTH_A, 3 * WIDTH_A + WIDTH_B], axis=-1)
    q = q.reshape(B, N, N_HEADS_A, HEAD_DIM)
    k = k.reshape(B, N, N_HEADS_A, HEAD_DIM)
    v = v.reshape(B, N, N_HEADS_A, HEAD_DIM)
    o = jnp.concatenate([attn_fn(q, k, v), fourier_mix(f), gmlp_mix(g, gn, ws, gb)], axis=-1)
    x = x + mod[:, :, 5] * rmsnorm(o @ w_out, npost[1])

    h = modulate(x, npre[2], mod[:, :, 6], mod[:, :, 7])
    x = x + 0.5 * mod[:, :, 8] * rmsnorm(swiglu(h, fw_in[1], fw_out[1]), npost[2])
    return x, k, v


def setup_inputs(seed: int = 0) -> dict:
    key = jax.random.key(seed)
    ks = jax.random.split(key, 20)
    nrm = jax.random.normal
    f32 = jnp.float32
    return {
        'x_prompt': nrm(ks[0], (BATCH, SEQ, D_MODEL), f32),
        'x_sample': nrm(ks[1], (DEC_BATCH, DEC_SEQ, D_MODEL), f32),
        'cache_k': nrm(ks[2], (DEC_BATCH, DEPTH, PAST_LEN, N_HEADS_A, HEAD_DIM), f32),
        'cache_v': nrm(ks[3], (DEC_BATCH, DEPTH, PAST_LEN, N_HEADS_A, HEAD_DIM), f32),
        'c': nrm(ks[4], (DEC_BATCH, D_MODEL), f32),
        'c_ctx': nrm(ks[5], (D_MODEL,), f32),
        'ada_w': nrm(ks[6], (DEPTH, D_MODEL, 3 * N_SUB * D_MODEL), f32) * (0.5 * D_MODEL ** -0.5),
        'ada_b': nrm(ks[7], (DEPTH, 3 * N_SUB * D_MODEL), f32) * 0.02,
        'norm_pre': 1.0 + 0.01 * nrm(ks[8], (DEPTH, N_SUB, D_MODEL), f32),
        'norm_post': 1.0 + 0.01 * nrm(ks[9], (DEPTH, N_SUB, D_MODEL), f32),
        'ffn_w_in': nrm(ks[10], (DEPTH, 2, D_MODEL, 2 * D_FF), f32) * D_MODEL ** -0.5,
        'ffn_w_out': nrm(ks[11], (DEPTH, 2, D_FF, D_MODEL), f32) * D_FF ** -0.5,
        'w_in': nrm(ks[12], (DEPTH, D_MODEL, PROJ_WIDTH), f32) * D_MODEL ** -0.5,
        'w_out': nrm(ks[13], (DEPTH, MIX_WIDTH, D_MODEL), f32) * MIX_WIDTH ** -0.5,
        'rpb': nrm(ks[14], (DEPTH, N_HEADS_A, 2 * WIN_ROWS_MAX - 1, 2 * WIN_COLS - 1), f32) * 0.1,
        'gmlp_norm': 1.0 + 0.01 * nrm(ks[15], (DEPTH, N_GROUPS_C, GROUP_C), f32),
        'gmlp_w': nrm(ks[16], (DEPTH, N_GROUPS_C, CHUNK, CHUNK), f32) * CHUNK ** -0.5,
        'gmlp_b': 1.0 + 0.01 * nrm(ks[17], (DEPTH, N_GROUPS_C, CHUNK), f32),
    }


def reference(x_prompt, x_sample, cache_k, cache_v, c, c_ctx, ada_w, ada_b, norm_pre, norm_post,
              ffn_w_in, ffn_w_out, w_in, w_out, rpb, gmlp_norm, gmlp_w, gmlp_b):
    xp = x_prompt
    xs = x_sample
    new_ks = []
    new_vs = []
    for l in range(DEPTH):
        mod_ctx = (jax.nn.silu(c_ctx) @ ada_w[l] + ada_b[l]).reshape(1, 1, 3 * N_SUB, D_MODEL)
        mod_lat = (jax.nn.silu(c) @ ada_w[l] + ada_b[l]).reshape(-1, 1, 3 * N_SUB, D_MODEL)
        xp, k_l, v_l = trunk_layer(xp, mod_ctx, norm_pre[l], norm_post[l], ffn_w_in[l], ffn_w_out[l],
                                   w_in[l], w_out[l], gmlp_norm[l], gmlp_w[l], gmlp_b[l],
                                   context_attention)
        new_ks.append(k_l)
        new_vs.append(v_l)
        ck = cache_k[:, l]
        cv = cache_v[:, l]
        rp = rpb[l]
        xs, _, _ = trunk_layer(xs, mod_lat, norm_pre[l], norm_post[l], ffn_w_in[l], ffn_w_out[l],
                               w_in[l], w_out[l], gmlp_norm[l], gmlp_w[l], gmlp_b[l],
                               lambda q, k, v: neighbourhood_attention(q, k, v, ck, cv, rp))
    new_k = jnp.stack(new_ks, axis=1)
    new_v = jnp.stack(new_vs, axis=1)
    return (xp, xs, new_k, new_v)
```

```python
import numpy as np
from contextlib import ExitStack
import concourse.bass as bass
import concourse.mybir as mybir
from concourse.bass_utils import run_bass_kernel_spmd

F32 = mybir.dt.float32
BF16 = mybir.dt.bfloat16
AF = mybir.ActivationFunctionType
ALU = mybir.AluOpType
AX = mybir.AxisListType

D = 1024
T = 1024
DFF = 2816
NFC = 22
NH = 8
NSLOT = 5
TH_OUTER_FFN = False
SLOT = 4096
KW = 896
EPS = 1e-6
NVL = 120
ENGS = ('pe', 'act', 'dve', 'pool', 'sp')


class Prog:
    def __init__(self):
        self.reset(True)
        self.tiles = []

    def reset(self, dry):
        self.dry = dry
        self.ops = {e: [] for e in ENGS}
        self.cnt = {e: 0 for e in ENGS}
        self.seen = {e: {} for e in ENGS}
        self.bw = {}
        self.br = {}
        self.dcnt = {}
        self.ntile = 0
        self.nloaded = 0
        self.psn = 0
        self.tmpn = 0
        self.out_tickets = {}
        self.nreleased = 0
        self.banks = list(range(8))

    def _deps(self, eng, reads, writes):
        deps = {}

        def add(t):
            if t is None:
                return
            s, v = t
            if deps.get(s, 0) < v:
                deps[s] = v
        for k in reads:
            add(self.bw.get(k))
        for k in writes:
            add(self.bw.get(k))
            for s, v in self.br.get(k, {}).items():
                add((s, v))
        waits = []
        for s, v in deps.items():
            if s == 'pe' and eng == 'pe':
                continue
            if self.seen[eng].get(s, 0) < v:
                self.seen[eng][s] = v
                waits.append((s, v))
        return waits

    def _commit(self, ticket, reads, writes):
        for k in writes:
            self.bw[k] = ticket
            self.br[k] = {}
        s, v = ticket
        for k in reads:
            d = self.br.setdefault(k, {})
            if d.get(s, 0) < v:
                d[s] = v

    def op(self, eng, fn, reads=(), writes=()):
        waits = self._deps(eng, reads, writes)
        self.cnt[eng] += 1
        ticket = (eng, self.cnt[eng])
        self.ops[eng].append((waits, fn, eng, 1))
        self._commit(ticket, reads, writes)
        return ticket

    def pe(self, fns, reads=(), writes=(), fine=None):
        waits = self._deps('pe', reads, writes)
        self.cnt['pe'] += 1
        ticket = ('pe', self.cnt['pe'])
        n = len(fns)
        allreads = tuple(reads)
        for i, fn in enumerate(fns):
            w = waits if i == 0 else []
            if fine is not None:
                w = w + self._deps('pe', fine[i], ())
                allreads = allreads + tuple(fine[i])
            self.ops['pe'].append((w, fn, 'pe' if i == n - 1 else None, 1))
        self._commit(ticket, allreads, writes)
        return ticket

    def dma(self, q, fn, dsem, reads=(), writes=()):
        waits = self._deps(q, reads, writes)
        self.dcnt[dsem] = self.dcnt.get(dsem, 0) + 16
        ticket = (dsem, self.dcnt[dsem])
        self.ops[q].append((waits, fn, dsem, 16))
        self._commit(ticket, reads, writes)
        return ticket

    def wait_all(self, eng, tickets):
        waits = []
        for s, v in tickets:
            if self.seen[eng].get(s, 0) < v:
                self.seen[eng][s] = v
                waits.append((s, v))
        self.ops[eng].append((waits, None, None, 0))

    def psum(self, n=1):
        b = self.banks[self.psn % len(self.banks)]
        self.psn += 1
        return b

    def tmp(self):
        i = self.tmpn % 3
        self.tmpn += 1
        return i


def build(NL=4, DBG=False):
    nc = bass.Bass("TRN2", target_bir_lowering=False)

    def din(name, shape):
        return nc.dram_tensor(name, shape, F32, kind="ExternalInput").ap()

    def dout(name, shape):
        return nc.dram_tensor(name, shape, F32, kind="ExternalOutput").ap()

    x_d = din("x", [T, D])
    vecs_d = din("vecs", [128, NL * NVL + 16])
    adaw_d = din("ada_w", [NL, D, 9 * D])
    fwi_d = din("ffn_w_in", [NL, 2, D, 2 * DFF])
    fwo_d = din("ffn_w_out", [NL, 2, DFF, D])
    wi_d = din("w_in", [NL, D, 2304])
    wo_d = din("w_out", [NL, D, D])
    ck_d = din("ck", [NL, 256, 512])
    cv_d = din("cv", [NL, 256, 512])
    tb_d = din("tb", [NL, NH, 128, 9 * 128])
    mk_d = din("mk", [128, 1024 + 1280])
    dft_d = din("dft", [2, T, T])
    cc_d = din("cc", [128, 256])
    gw_d = din("gwT", [NL, 128, 4 * 128])
    gb_d = din("gbb", [NL, 128, 2 * 128])
    id_d = din("ident", [128, 128])
    y_d = dout("y", [T, D])
    nk_d = dout("nk", [NL, T, 512])
    nv_d = dout("nv", [NL, T, 512])
    dbg_d = dout("dbg", [128, 8 * T]) if DBG else None

    P = Prog()
    es = ExitStack()

    def sb(name, shape, dt):
        return es.enter_context(nc.sbuf_tensor("sb_" + name, shape, dt))

    xT = sb("xT", [128, 8, T], F32)
    Y = sb("Y", [128, 8, 2, 512], F32)
    actT = sb("actT", [128, NFC, T], BF16)
    ring = sb("ring", [128, NSLOT, SLOT], BF16)
    tmpf = sb("tmpf", [128, 3, 512], F32)
    stage = sb("stage", [128, 2, 512], F32)
    Tt = sb("Tt", [128, 2, 9 * 128], BF16)
    qz = sb("qz", [128, 4, 128], BF16)
    Pb = sb("Pb", [128, 2, KW], BF16)
    PT = sb("PT", [128, 2, KW], BF16)
    V = sb("V", [128, 8, 512], BF16)
    PQO = sb("PQO", [128, 8, 512], BF16)
    kctok = sb("kctok", [128, 2, 512], BF16)
    vc = sb("vc", [128, 2, 512], BF16)
    kcT = sb("kcT", [128, 4, 256], BF16)
    mk = sb("mk", [128, 1024 + 1280], BF16)
    ccs = sb("ccs", [128, 256], BF16)
    gwT = sb("gwT_s", [128, 4, 128], BF16)
    gbb = sb("gbb_s", [128, 2, 128], F32)
    vecs = sb("vecs_s", [128, NL * NVL + 16], F32)
    MOD = sb("MOD", [128, 3, 72], F32)
    ABG = sb("ABG", [128, 3, 24], F32)
    scb = sb("scb", [128, 8], BF16)
    ident_f = sb("ident_f", [128, 128], F32)
    ident_b = sb("ident_b", [128, 128], BF16)
    ones_b = sb("ones_b", [128, 128], BF16)
    epsc = sb("epsc", [128, 1], F32)
    st = sb("st", [128, 160], F32)
    gvt = sb("gvt", [128, 2, 256], F32)
    ps = es.enter_context(nc.psum_tensor("psum_all", [128, 8, 512], F32))

    def hT(dc, th):
        return Y[:, dc, th, 0:256].bitcast(BF16)

    def sqv(dc):
        c, th = dc // 2, dc % 2
        return actT[:, c, th * 512:(th + 1) * 512]

    def kY(dc, th):
        return ('Y', dc, th)

    def kA(c, th):
        return ('act', c, th)

    def kX(dc, th):
        return ('x', dc, th)

    def psb(b):
        return ps[:, b, :]

    def psb16(b):
        return ps[:, b, :].bitcast(BF16)

    def emit_load(n):
        spec = P.tiles[n]
        slot = n % NSLOT
        kind = spec[0]
        sview = ring[:, slot, :]
        if kind == 'A8':
            _, parts, Wt = spec
            for pi, (src, c0, w, d0) in enumerate(parts):
                dst = sview[:, 0:8 * Wt].rearrange("p (c w) -> p c w", c=8)[:, :, d0:d0 + w]
                srcap = src()[:, c0:c0 + w].rearrange("(c p) w -> p c w", p=128)
                tk = P.dma('pool', (lambda e, dst=dst, srcap=srcap: e.dma_start(out=dst, in_=srcap)),
                           'ring%d' % slot, reads=(), writes=(('ring', slot, pi),))
            for pi in range(2):
                P.bw[('ring', slot, pi)] = tk
                P.br.setdefault(('ring', slot, pi), {})
        elif kind == 'WO':
            _, src, nf, d0 = spec
            dst = sview[:, 0:nf * 128].rearrange("p (c w) -> p c w", c=nf)
            srcap = src()[:, d0:d0 + 128].rearrange("(c p) w -> p c w", p=128)
            P.dma('pool', (lambda e, dst=dst, srcap=srcap: e.dma_start(out=dst, in_=srcap)),
                  'ring%d' % slot, reads=(), writes=(('ring', slot, 0), ('ring', slot, 1)))

    def _fill():
        while P.nloaded < min(len(P.tiles), P.nreleased + NSLOT):
            emit_load(P.nloaded)
            P.nloaded += 1

    def acquire(spec):
        n = P.ntile
        P.ntile += 1
        if P.dry:
            P.tiles.append(spec)
        else:
            _fill()
            assert n < P.nloaded
        return n % NSLOT

    def release():
        P.nreleased += 1
        if not P.dry:
            _fill()

    def kR(slot):
        return (('ring', slot, 0), ('ring', slot, 1))

    def ring8(slot, W=512):
        return ring[:, slot, 0:8 * W].rearrange("p (c w) -> p c w", c=8)

    def ringwo(slot, nf):
        return ring[:, slot, 0:nf * 128].rearrange("p (c w) -> p c w", c=nf)

    def mm(out, lhsT, rhs, start, stop):
        return lambda e: e.matmul(out, lhsT=lhsT, rhs=rhs, start=start, stop=stop)

    def tr(out, in_, ident):
        return lambda e: e.transpose(out, in_, ident)

    def act(out, in_, func, bias=None, scale=None, accum=None):
        kw = {}
        if bias is not None:
            kw['bias'] = bias
        if scale is not None:
            kw['scale'] = scale
        if accum is not None:
            kw['accum_out'] = accum
        return lambda e: e.activation(out=out, in_=in_, func=func, **kw)

    def tt(out, in0, in1, op):
        return lambda e: e.tensor_tensor(out=out, in0=in0, in1=in1, op=op)

    def ts(out, in0, s1, op0, s2=None, op1=None):
        if op1 is None:
            return lambda e: e.tensor_scalar(out=out, in0=in0, scalar1=s1, scalar2=None, op0=op0)
        return lambda e: e.tensor_scalar(out=out, in0=in0, scalar1=s1, scalar2=s2, op0=op0, op1=op1)

    def stt(out, in0, scalar, in1, op0, op1):
        return lambda e: e.scalar_tensor_tensor(out=out, in0=in0, scalar=scalar, in1=in1, op0=op0, op1=op1)

    def cp(out, in_):
        return lambda e: e.tensor_copy(out=out, in_=in_)

    def xstage(i):
        return actT[:, 2 * i:2 * i + 2, :].rearrange("p c t -> p (c t)").bitcast(F32)

    def kXS(i):
        return (kA(2 * i, 0), kA(2 * i, 1), kA(2 * i + 1, 0), kA(2 * i + 1, 1))

    def emit_all2():
        P.nreleased = 0
        cvo = NL * NVL
        P.dma('sp', lambda e: e.dma_start(out=vecs[:], in_=vecs_d), 'c_vecs', writes=('vecs',))
        P.dma('sp', lambda e: e.dma_start(out=ident_f[:], in_=id_d), 'c_id', writes=('identf',))
        P.dma('pool', lambda e: e.dma_start(out=mk[:, 0:1024], in_=mk_d[:, 0:1024]), 'c_mk', writes=('mk',))
        P.dma('pool', lambda e: e.dma_start(out=mk[:, 1024:2304], in_=mk_d[:, 1024:2304]), 'c_mk2', writes=('mk2',))
        P.dma('pool', lambda e: e.dma_start(out=ccs[:], in_=cc_d), 'c_cc', writes=('ccs',))
        P.op('dve', cp(ident_b[:], ident_f[:]), reads=('identf',), writes=('identb',))
        P.op('dve', lambda e: e.memset(ones_b[:], 1.0), writes=('ones',))
        P.op('dve', lambda e: e.memset(epsc[:], EPS), writes=('eps',))
        P.op('dve', lambda e: e.memset(qz[:], 0.0), writes=tuple(('qz', j) for j in range(4)))
        P.op('act', act(scb[:], vecs[:, cvo:cvo + 8], AF.Silu), reads=('vecs',), writes=('scb',))

        BODY = [0, 1, 2, 4, 5, 6, 7]
        BND = [0, 1]
        BODY_A = [2, 3, 6, 7]
        BODY_B = [2, 4, 6, 7]

        def mod_gen(l, s):
            mb = (l * 3 + s) % 3
            b = 3
            for t6 in range(6):
                c0 = s * 3072 + t6 * 512
                slot = acquire(('A8', [(lambda l=l: adaw_d[l], c0, 512, 0)], 512))
                w8 = ring8(slot)
                fns = []
                for c4 in range(4):
                    col = t6 * 4 + c4
                    for dc in range(8):
                        fns.append(mm(ps[:, b, col:col + 1], w8[:, dc, c4 * 128:(c4 + 1) * 128],
                                      scb[:, dc:dc + 1], dc == 0, dc == 7))
                P.pe(fns, reads=kR(slot) + ('scb',), writes=(('ps', b),))
                release()
                vo = l * NVL + s * 24
                m0 = s * 24
                if t6 == 3:
                    npre = l * NVL + 72 + s * 8
                    P.op('dve', tt(MOD[:, mb, m0:m0 + 16], ps[:, b, 0:16], vecs[:, vo:vo + 16], ALU.add),
                         reads=(('ps', b), 'vecs'), writes=(('mod', mb),))
                    P.op('dve', stt(ABG[:, mb, 0:8], MOD[:, mb, m0 + 8:m0 + 16], 1.0, vecs[:, npre:npre + 8], ALU.add, ALU.mult),
                         reads=(('mod', mb), 'vecs'), writes=(('abgA', mb),))
                    P.op('dve', cp(ABG[:, mb, 8:16], MOD[:, mb, m0:m0 + 8]), reads=(('mod', mb),), writes=(('abgB', mb),))
                if t6 == 5:
                    npost = l * NVL + 96 + s * 8
                    coef = 1.0 if s == 1 else 0.5
                    P.op('dve', tt(MOD[:, mb, m0 + 16:m0 + 24], ps[:, b, 16:24], vecs[:, vo + 16:vo + 24], ALU.add),
                         reads=(('ps', b), 'vecs'), writes=(('modg', mb),))
                    P.op('dve', stt(ABG[:, mb, 16:24], MOD[:, mb, m0 + 16:m0 + 24], coef, vecs[:, npost:npost + 8], ALU.mult, ALU.mult),
                         reads=(('modg', mb), 'vecs'), writes=(('abgG', mb),))
                yield

        MG = {'q': [], 'busy': False}

        def mod_step():
            while MG['q']:
                try:
                    next(MG['q'][0])
                    break
                except StopIteration:
                    MG['q'].pop(0)
            MG['busy'] = len(MG['q']) > 1

        def mod_drain():
            while MG['q']:
                mod_step()

        def emit_mod(l, s):
            MG['q'].append(mod_gen(l, s))
            mod_drain()
            return (l * 3 + s) % 3

        def sqY(dc, th):
            return V[:, dc, :] if th == 0 else PQO[:, dc, :]

        def kSQ(dc, th):
            return ('V', dc) if th == 0 else ('pqo', dc)

        def stats_to_rstd(sq_fn, sq_keys, rb):
            b = P.psum()
            fns = [mm(psb(b), ones_b[:], sq_fn(dc), dc == 0, dc == 7) for dc in range(8)]
            P.pe(fns, reads=('ones',), writes=(('ps', b),), fine=[(sq_keys(dc),) for dc in range(8)])
            t1 = P.tmp()
            P.op('act', act(tmpf[:, t1, :], psb(b), AF.Ln, bias=epsc[:], scale=1.0 / D),
                 reads=(('ps', b), 'eps'), writes=(('tmp', t1),))
            P.op('act', act(psb(rb), tmpf[:, t1, :], AF.Exp, scale=-0.5),
                 reads=(('tmp', t1),), writes=(('ps', rb),))

        def norm_modulate(mb, th, phase='all'):
            rb = 4 + th
            sqf = (lambda dc: V[:, dc, :]) if th == 0 else (lambda dc: PQO[:, dc, :])
            sqk = (lambda dc: ('V', dc)) if th == 0 else (lambda dc: ('pqo', dc))
            if phase in ('all', 'sq'):
                for dc in range(8):
                    P.op('act', act(sqf(dc), xT[:, dc, th * 512:(th + 1) * 512], AF.Square), reads=(kX(dc, th),), writes=(sqk(dc),))
            if phase == 'sq':
                return
            stats_to_rstd(sqf, sqk, rb)
            for dc in range(8):
                t2 = P.tmp()
                P.op('dve', tt(tmpf[:, t2, :], xT[:, dc, th * 512:(th + 1) * 512], psb(rb), ALU.mult),
                     reads=(kX(dc, th), ('ps', rb)), writes=(('tmp', t2),))
                P.op('act', act(hT(dc, th), tmpf[:, t2, :], AF.Identity, bias=ABG[:, mb, 8 + dc:9 + dc],
                                scale=ABG[:, mb, dc:dc + 1]),
                     reads=(('tmp', t2), ('abgA', mb), ('abgB', mb)), writes=(kY(dc, th),))

        def post_stats(th):
            stats_to_rstd(lambda dc: sqY(dc, th), lambda dc: kSQ(dc, th), 4 + th)

        def post_update(mb, th, dcs=range(8)):
            rb = 4 + th
            dcs = list(dcs)
            for p0 in range(0, len(dcs), 2):
                pair = dcs[p0:p0 + 2]
                banks = {}
                for dc in pair:
                    tbk = P.psum()
                    banks[dc] = tbk
                    P.op('dve', tt(psb(tbk), Y[:, dc, th, :], psb(rb), ALU.mult),
                         reads=(kY(dc, th), ('ps', rb)), writes=(('ps', tbk),))
                for dc in pair:
                    tbk = banks[dc]
                    xs = xT[:, dc, th * 512:(th + 1) * 512]
                    P.op('dve', stt(xs, psb(tbk), ABG[:, mb, 16 + dc:17 + dc], xs, ALU.mult, ALU.add),
                         reads=(('ps', tbk), ('abgG', mb), kX(dc, th)), writes=(kX(dc, th),))

        def out_proj(src_fn, nf, in_view, in_keys, hook_every=0, after_th0=None, late_th1=None, th_outer=True):
            if not th_outer:
                LAG = 3
                slots = {}

                def group(dc, th):
                    w = ringwo(slots[dc], nf)
                    b = P.psum()
                    fns = [mm(psb(b), w[:, fc, :], in_view(fc, th), fc == 0, fc == nf - 1) for fc in range(nf)]
                    P.pe(fns, reads=kR(slots[dc]), writes=(('ps', b),), fine=[(in_keys(fc, th),) for fc in range(nf)])
                    P.op('dve', cp(Y[:, dc, th, :], psb(b)), reads=(('ps', b),), writes=(kY(dc, th),))
                    P.op('act', act(sqY(dc, th), Y[:, dc, th, :], AF.Square), reads=(kY(dc, th),), writes=(kSQ(dc, th),))
                for t in range(8 + LAG):
                    if t < 8:
                        slots[t] = acquire(('WO', src_fn, nf, t * 128))
                        group(t, 0)
                        if t == 7 and after_th0 is not None:
                            after_th0()
                    if t - LAG >= 0:
                        group(t - LAG, 1)
                        release()
                        if t - LAG == 6 and late_th1 is not None:
                            late_th1()
                return
            for th in range(2):
                for dc in range(8):
                    slot = acquire(('WO', src_fn, nf, dc * 128))
                    w = ringwo(slot, nf)
                    b = P.psum()
                    fns = [mm(psb(b), w[:, fc, :], in_view(fc, th), fc == 0, fc == nf - 1) for fc in range(nf)]
                    P.pe(fns, reads=kR(slot) + tuple(in_keys(fc, th) for fc in range(nf)), writes=(('ps', b),))
                    P.op('dve', cp(Y[:, dc, th, :], psb(b)), reads=(('ps', b),), writes=(kY(dc, th),))
                    P.op('act', act(sqY(dc, th), Y[:, dc, th, :], AF.Square), reads=(kY(dc, th),), writes=(kSQ(dc, th),))
                    release()
                    if th == 0 and hook_every and ((dc % hook_every) == hook_every - 1 or MG['busy']):
                        mod_step()
                    if th == 1 and dc == 3 and late_th1 is not None:
                        late_th1()
                if th == 0:
                    mod_drain()
                    if after_th0 is not None:
                        after_th0()

        def layer_loads(l):
            P.dma('pool', lambda e: e.dma_start(out=kctok[:], in_=ck_d[l].rearrange("(b p) f -> p b f", p=128)),
                  'c_kc', writes=('kctok',))
            P.dma('pool', lambda e: e.dma_start(out=vc[:], in_=cv_d[l].rearrange("(b p) f -> p b f", p=128)),
                  'c_vc', writes=('vc',))
            P.dma('pool', lambda e: e.dma_start(out=gwT[:], in_=gw_d[l].rearrange("p (g q) -> p g q", g=4)),
                  'c_gw', writes=('gwT',))
            P.dma('sp', lambda e: e.dma_start(out=gbb[:], in_=gb_d[l].rearrange("p (g q) -> p g q", g=2)),
                  'c_gb', writes=('gbb',))

        def kct_transposes():
            b = P.psum()
            fns = []
            for kb in range(2):
                for c in range(4):
                    fns.append(tr(psb16(b)[:, (c * 2 + kb) * 128:(c * 2 + kb + 1) * 128],
                                  kctok[:, kb, c * 128:(c + 1) * 128], ident_b[:]))
            P.pe(fns, reads=('kctok', 'identb'), writes=(('ps', b),))
            P.op('dve', cp(kcT[:].rearrange("p c k -> p (c k)"), psb16(b)), reads=(('ps', b),), writes=('kcT',))

        def ffn(l, i, mid, cb):
            def grp(slot, g, jj, th):
                w8 = ring8(slot)
                j = 2 * g + jj
                bg = P.psum()
                bu = P.psum()
                rk = kR(slot)
                fine = [(kY(dc, th),) for dc in range(8)]
                P.pe([mm(psb(bg), w8[:, dc, jj * 128:(jj + 1) * 128], hT(dc, th), dc == 0, dc == 7)
                      for dc in range(8)], reads=rk, writes=(('ps', bg),), fine=fine)
                P.pe([mm(psb(bu), w8[:, dc, 256 + jj * 128:256 + (jj + 1) * 128], hT(dc, th), dc == 0, dc == 7)
                      for dc in range(8)], reads=rk, writes=(('ps', bu),), fine=fine)
                t1 = P.tmp()
                P.op('act', act(tmpf[:, t1, :], psb(bg), AF.Silu), reads=(('ps', bg),), writes=(('tmp', t1),))
                P.op('dve', tt(actT[:, j, th * 512:(th + 1) * 512], tmpf[:, t1, :], psb(bu), ALU.mult),
                     reads=(('tmp', t1), ('ps', bu)), writes=(kA(j, th),))

            def wspec(g):
                return ('A8', [(lambda l=l, i=i: fwi_d[l, i], 256 * g, 256, 0),
                               (lambda l=l, i=i: fwi_d[l, i], DFF + 256 * g, 256, 256)], 512)
            s0 = acquire(wspec(0))
            s1 = acquire(wspec(1))
            s2 = acquire(wspec(2))
            for th in range(2):
                for g, slot in ((0, s0), (1, s1), (2, s2)):
                    for jj in range(2):
                        grp(slot, g, jj, th)
                    if th == 0 and g == 1:
                        mid()
            release()
            release()
            release()
            for g in range(3, 11):
                slot = acquire(wspec(g))
                for jj in range(2):
                    for th in range(2):
                        grp(slot, g, jj, th)
                release()
                if TH_OUTER_FFN:
                    if g % 3 == 1 or MG['busy']:
                        mod_step()
                else:
                    mod_step()
                    if MG['busy']:
                        mod_step()
                if g == 6 and i == 0:
                    kct_transposes()
            if not TH_OUTER_FFN:
                mod_drain()
            out_proj(lambda l=l, i=i: fwo_d[l, i], NFC,
                     lambda fc, th: actT[:, fc, th * 512:(th + 1) * 512], lambda fc, th: kA(fc, th), hook_every=2,
                     after_th0=cb[0], late_th1=cb[1], th_outer=TH_OUTER_FFN)

        def qT(c): return actT[:, c, :]
        def kTt(c): return actT[:, 4 + c, :]
        def fTt(c): return actT[:, 8 + c, :]
        def uTt(c): return actT[:, 10 + c, :]
        def oTt(c): return actT[:, 12 + c, :]
        vn = actT[:, 20:22, :].rearrange("p c t -> p (c t)").rearrange("p (b w) -> p b w", b=8)

        def kVN(tb):
            return kA(20 + tb // 4, (tb % 4) // 2)

        def mixer(l, mid, cb):
            hk = lambda th: tuple(kY(dc, th) for dc in range(8))
            slot = acquire(('A8', [(lambda l=l: wi_d[l], 0, 512, 0)], 512))
            w8 = ring8(slot)
            for th in range(2):
                for c in range(4):
                    b = P.psum()
                    P.pe([mm(psb(b), w8[:, dc, c * 128:(c + 1) * 128], hT(dc, th), dc == 0, dc == 7) for dc in range(8)],
                         reads=kR(slot), writes=(('ps', b),), fine=[(kY(dc, th),) for dc in range(8)])
                    P.op('act', act(qT(c)[:, th * 512:(th + 1) * 512], psb(b), AF.Identity, scale=0.125),
                         reads=(('ps', b),), writes=(kA(c, th),))
                if th == 0:
                    mid()
            release()
            mod_step()
            slot = acquire(('A8', [(lambda l=l: wi_d[l], 1536, 512, 0)], 512))
            w8 = ring8(slot)
            for c in range(4):
                for th in range(2):
                    b = P.psum()
                    P.pe([mm(psb(b), w8[:, dc, c * 128:(c + 1) * 128], hT(dc, th), dc == 0, dc == 7) for dc in range(8)],
                         reads=kR(slot) + hk(th), writes=(('ps', b),))
                    if c < 2:
                        P.op('dve', cp(fTt(c)[:, th * 512:(th + 1) * 512], psb(b)), reads=(('ps', b),), writes=(kA(8 + c, th),))
                    else:
                        P.op('act', act(uTt(c - 2)[:, th * 512:(th + 1) * 512], psb(b), AF.Gelu_apprx_tanh),
                             reads=(('ps', b),), writes=(kA(10 + c - 2, th),))
            release()
            mod_step()
            for tb in range(8):
                th, tc = tb // 4, (tb % 4) * 128
                b = P.psum()
                fns = [mm(ps[:, b, k2 * 256:(k2 + 1) * 256], fTt(k2)[:, tb * 128:(tb + 1) * 128], ccs[:], True, True) for k2 in range(2)]
                P.pe(fns, reads=(kA(8, th), kA(9, th), 'ccs'), writes=(('ps', b),))
                P.op('act', act(PQO[:, tb, :], psb(b), AF.Identity), reads=(('ps', b),), writes=(('pqo', tb),))

            slot = acquire(('A8', [(lambda l=l: wi_d[l], 2048, 256, 0)], 256))
            w8 = ring8(slot, 256)
            gno = NL * NVL + 8
            for tb in range(8):
                th, tc = tb // 4, (tb % 4) * 128
                b = P.psum()
                P.pe([mm(ps[:, b, 0:256], hT(dc, th)[:, tc:tc + 128], w8[:, dc, :], dc == 0, dc == 7) for dc in range(8)],
                     reads=kR(slot) + hk(th), writes=(('ps', b),))
                g2 = tb % 2
                P.op('act', act(gvt[:, g2, :], ps[:, b, 0:256], AF.Gelu_apprx_tanh), reads=(('ps', b),), writes=(('gvt', g2),))
                so = g2 * 8
                P.op('dve', lambda e, g2=g2, so=so: e.tensor_reduce(out=st[:, so:so + 4], in_=gvt[:, g2, :].rearrange("p (g c) -> p g c", g=4),
                                                                     axis=AX.X, op=ALU.add),
                     reads=(('gvt', g2),), writes=(('st', g2),))
                P.op('dve', ts(st[:, so + 4:so + 8], st[:, so:so + 4], -1.0 / 64, ALU.mult), reads=(('st', g2),), writes=(('st', g2),))
                P.op('dve', tt(Y[:, tb, 0, 256:512].rearrange("p (g c) -> p g c", g=4), gvt[:, g2, :].rearrange("p (g c) -> p g c", g=4),
                               st[:, so + 4:so + 8].unsqueeze(2).broadcast_to([128, 4, 64]), ALU.add),
                     reads=(('gvt', g2), ('st', g2)), writes=(('cen', tb),))
                P.op('dve', tt(gvt[:, g2, :], Y[:, tb, 0, 256:512], Y[:, tb, 0, 256:512], ALU.mult), reads=(('cen', tb),), writes=(('gvt', g2),))
                P.op('dve', lambda e, g2=g2, tb=tb: e.tensor_reduce(out=st[:, 128 + tb * 4:132 + tb * 4], in_=gvt[:, g2, :].rearrange("p (g c) -> p g c", g=4),
                                                                     axis=AX.X, op=ALU.add),
                     reads=(('gvt', g2),), writes=(('lnss',),))
            release()
            mod_step()

            slot = acquire(('A8', [(lambda l=l: wi_d[l], 512, 512, 0)], 512))
            w8 = ring8(slot)
            for c in range(4):
                for th in range(2):
                    b = P.psum()
                    P.pe([mm(psb(b), w8[:, dc, c * 128:(c + 1) * 128], hT(dc, th), dc == 0, dc == 7) for dc in range(8)],
                         reads=kR(slot) + hk(th), writes=(('ps', b),))
                    P.op('dve', cp(kTt(c)[:, th * 512:(th + 1) * 512], psb(b)), reads=(('ps', b),), writes=(kA(4 + c, th),))
            for tb in range(8):
                th, tc = tb // 4, (tb % 4) * 128
                b = P.psum()
                P.pe([mm(psb(b), hT(dc, th)[:, tc:tc + 128], w8[:, dc, :], dc == 0, dc == 7) for dc in range(8)],
                     reads=kR(slot) + hk(th), writes=(('ps', b),))
                sg = tb % 2
                P.op('act', act(stage[:, sg, :], psb(b), AF.Identity), reads=(('ps', b),), writes=(('stage', sg),))
                t = P.dma('sp', lambda e, tb=tb, sg=sg: e.dma_start(out=nk_d[l, tb * 128:(tb + 1) * 128, :], in_=stage[:, sg, :]),
                          'o_st%d' % sg, reads=(('stage', sg),))
                P.out_tickets[t[0]] = t
            release()
            mod_step()
            s_c = acquire(('A8', [(lambda: dft_d[0], 0 * 512, 512, 0)], 512))
            s_s = acquire(('A8', [(lambda: dft_d[1], 0 * 512, 512, 0)], 512))
            wc, wsn = ring8(s_c), ring8(s_s)
            for k2 in range(2):
                b = P.psum()
                fns = []
                for nb in range(8):
                    fns.append(mm(psb(b), PQO[:, nb, k2 * 256:k2 * 256 + 128], wc[:, nb, :], nb == 0, False))
                    fns.append(mm(psb(b), PQO[:, nb, k2 * 256 + 128:k2 * 256 + 256], wsn[:, nb, :], False, nb == 7))
                P.pe(fns, reads=kR(s_c) + kR(s_s) + tuple(('pqo', nb) for nb in range(8)), writes=(('ps', b),))
                P.op('act', act(oTt(4 + k2)[:, 0 * 512:(0 + 1) * 512], psb(b), AF.Identity), reads=(('ps', b),), writes=(kA(16 + k2, 0),))
            release()
            release()
            mod_step()

            P.op('act', act(st[:, 128:160], st[:, 128:160], AF.Sqrt, bias=epsc[:], scale=1.0 / 64),
                 reads=(('lnss',), 'eps'), writes=(('lnss',),))
            P.op('dve', lambda e: e.reciprocal(out=st[:, 128:160], in_=st[:, 128:160]), reads=(('lnss',),), writes=(('lnss',),))
            for tb in range(8):
                P.op('dve', tt(vn[:, tb, :].rearrange("p (g c) -> p g c", g=4), Y[:, tb, 0, 256:512].rearrange("p (g c) -> p g c", g=4),
                               st[:, 128 + tb * 4:132 + tb * 4].unsqueeze(2).broadcast_to([128, 4, 64]), ALU.mult),
                     reads=(('cen', tb), ('lnss',)), writes=(kVN(tb),))

            slot = acquire(('A8', [(lambda l=l: wi_d[l], 1024, 512, 0)], 512))
            w8 = ring8(slot)
            for tb in range(8):
                th, tc = tb // 4, (tb % 4) * 128
                b = P.psum()
                P.pe([mm(psb(b), hT(dc, th)[:, tc:tc + 128], w8[:, dc, :], dc == 0, dc == 7) for dc in range(8)],
                     reads=kR(slot) + hk(th), writes=(('ps', b),))
                sg = tb % 2
                P.op('act', act(stage[:, sg, :], psb(b), AF.Identity), reads=(('ps', b),), writes=(('stage', sg),))
                P.op('dve', cp(V[:, tb, :], stage[:, sg, :]), reads=(('stage', sg),), writes=(('V', tb),))
                t = P.dma('sp', lambda e, tb=tb, sg=sg: e.dma_start(out=nv_d[l, tb * 128:(tb + 1) * 128, :], in_=stage[:, sg, :]),
                          'o_st%d' % sg, reads=(('stage', sg),))
                P.out_tickets[t[0]] = t
            release()
            mod_step()
            s_c = acquire(('A8', [(lambda: dft_d[0], 1 * 512, 512, 0)], 512))
            s_s = acquire(('A8', [(lambda: dft_d[1], 1 * 512, 512, 0)], 512))
            wc, wsn = ring8(s_c), ring8(s_s)
            for k2 in range(2):
                b = P.psum()
                fns = []
                for nb in range(8):
                    fns.append(mm(psb(b), PQO[:, nb, k2 * 256:k2 * 256 + 128], wc[:, nb, :], nb == 0, False))
                    fns.append(mm(psb(b), PQO[:, nb, k2 * 256 + 128:k2 * 256 + 256], wsn[:, nb, :], False, nb == 7))
                P.pe(fns, reads=kR(s_c) + kR(s_s) + tuple(('pqo', nb) for nb in range(8)), writes=(('ps', b),))
                P.op('act', act(oTt(4 + k2)[:, 1 * 512:(1 + 1) * 512], psb(b), AF.Identity), reads=(('ps', b),), writes=(kA(16 + k2, 1),))
            release()
            release()
            mod_step()

            for tb in range(8):
                th, tc = tb // 4, (tb % 4) * 128
                b = P.psum()
                fns = []
                for k2 in range(2):
                    for gg in range(2):
                        g = 2 * k2 + gg
                        fns.append(mm(ps[gg * 64:(gg + 1) * 64, b, k2 * 128:(k2 + 1) * 128], vn[:, tb, g * 64:(g + 1) * 64],
                                      gwT[:, g, :], True, True))
                P.pe(fns, reads=(kVN(tb), 'gwT'), writes=(('ps', b),))
                for k2 in range(2):
                    t1 = P.tmp()
                    P.op('dve', stt(tmpf[:, t1, 0:128], ps[:, b, k2 * 128:(k2 + 1) * 128], vecs[:, gno + l * 2 + k2:gno + l * 2 + k2 + 1],
                                    gbb[:, k2, :], ALU.mult, ALU.add),
                         reads=(('ps', b), 'vecs', 'gbb'), writes=(('tmp', t1),))
                    P.op('dve', tt(oTt(6 + k2)[:, tb * 128:(tb + 1) * 128], tmpf[:, t1, 0:128], uTt(k2)[:, tb * 128:(tb + 1) * 128], ALU.mult),
                         reads=(('tmp', t1), kA(10 + k2, th)), writes=(kA(18 + k2, th),))

            OH = mk[:, 0:1024]
            Wm = mk[:, 1024:2304]
            mod_step()
            mod_drain()

            def t_load(h):
                tbuf = h % 2
                P.dma('pool', lambda e, h=h, tbuf=tbuf: e.dma_start(out=Tt[:, tbuf, :], in_=tb_d[l, h]), 'c_T%d' % tbuf, writes=(('Tt', tbuf),))

            def geom(n):
                h, i = divmod(n, 8)
                lo = min(max(i - 2, 0), 3)
                return h, i, h // 2, (h % 2) * 64, lo, 2 * (n % 3), n % 2

            def kwin(i):
                if i <= 1:
                    return 0, 4
                if i >= 6:
                    return 4, 4
                return min(max(i - 2, 0), 3), 5

            def att_scores(n):
                h, i, hc, pb, lo, bs, sb2 = geom(n)
                tbuf = h % 2
                qi = (h % 2) * 2 + (i % 2)
                P.op('pool', cp(qz[pb:pb + 64, qi, :], actT[pb:pb + 64, hc, i * 128:(i + 1) * 128]),
                     reads=(kA(hc, i // 4),), writes=(('qz', qi),))
                qsl = qz[:, qi, :]
                ohs = OH[:, i * 128:(i + 1) * 128]
                lo, nkb = kwin(i)
                k0 = lo * 128
                d0 = (lo - i + 4) * 128
                kch = actT[:, 4 + hc, :]
                Sps = ps[:, bs:bs + 2, :].rearrange("p b w -> p (b w)")
                if nkb == 5:
                    fns = [
                        mm(Sps[:, 0:512], qsl, kch[:, k0:k0 + 512], True, False),
                        mm(Sps[:, 0:512], ohs, Wm[:, k0:k0 + 512], False, False),
                        mm(Sps[:, 0:512], ident_b[:], Tt[:, tbuf, d0:d0 + 512], False, True),
                        mm(Sps[:, 512:640], qsl, kch[:, k0 + 512:k0 + 640], True, False),
                        mm(Sps[:, 640:896], qsl, kcT[:, hc, :], False, False),
                        mm(Sps[:, 512:640], ohs, Wm[:, k0 + 512:k0 + 640], False, False),
                        mm(Sps[:, 640:896], ohs, Wm[:, 1024:1280], False, False),
                        mm(Sps[:, 512:640], ident_b[:], Tt[:, tbuf, d0 + 512:d0 + 640], False, True),
                    ]
                else:
                    fns = [
                        mm(Sps[:, 0:512], qsl, kch[:, k0:k0 + 512], True, False),
                        mm(Sps[:, 0:512], ohs, Wm[:, k0:k0 + 512], False, False),
                        mm(Sps[:, 0:512], ident_b[:], Tt[:, tbuf, d0:d0 + 512], False, True),
                        mm(Sps[:, 512:768], qsl, kcT[:, hc, :], True, False),
                        mm(Sps[:, 512:768], ohs, Wm[:, 1024:1280], False, True),
                    ]
                P.pe(fns, reads=(('qz', qi), kA(4 + hc, 0), kA(4 + hc, 1), 'kcT', 'mk', 'mk2', ('Tt', tbuf), 'identb'),
                     writes=(('ps', bs), ('ps', bs + 1)))

            def att_max(n):
                h, i, hc, pb, lo, bs, sb2 = geom(n)
                Sps = ps[:, bs:bs + 2, :].rearrange("p b w -> p (b w)")
                mo = 96 + (n % 3) * 4
                Wd = kwin(i)[1] * 128 + 256
                P.op('dve', lambda e: e.tensor_reduce(out=st[:, mo:mo + 1], in_=Sps[:, 0:Wd], axis=AX.X, op=ALU.max, negate=True),
                     reads=(('ps', bs), ('ps', bs + 1)), writes=(('stm', n % 3),))

            def att_exp(n):
                h, i, hc, pb, lo, bs, sb2 = geom(n)
                sso = 32 + (h % 2) * 32
                Sps = ps[:, bs:bs + 2, :].rearrange("p b w -> p (b w)")
                mo = 96 + (n % 3) * 4
                Wd = kwin(i)[1] * 128 + 256
                P.op('act', act(Pb[:, sb2, 0:Wd], Sps[:, 0:Wd], AF.Exp, bias=st[:, mo:mo + 1], scale=1.0,
                                accum=st[:, sso + i:sso + i + 1]),
                     reads=(('ps', bs), ('ps', bs + 1), ('stm', n % 3)), writes=(('Pb', sb2), ('ssum', h % 2)))

            def att_tr(n):
                h, i, hc, pb, lo, bs, sb2 = geom(n)
                bt = 6
                nt = kwin(i)[1] + 2
                Wd = nt * 128
                fns = [tr(psb16(bt)[:, kb * 128:(kb + 1) * 128], Pb[:, sb2, kb * 128:(kb + 1) * 128], ident_b[:]) for kb in range(nt)]
                P.pe(fns, reads=(('Pb', sb2), 'identb'), writes=(('ps', bt),))
                if n % 2 == 0:
                    P.op('act', act(PT[:, sb2, 0:Wd], psb16(bt)[:, 0:Wd], AF.Identity), reads=(('ps', bt),), writes=(('PT', sb2),))
                else:
                    P.op('dve', cp(PT[:, sb2, 0:Wd], psb16(bt)[:, 0:Wd]), reads=(('ps', bt),), writes=(('PT', sb2),))

            def att_pv(n):
                h, i, hc, pb, lo, bs, sb2 = geom(n)
                bo = 7
                fns = []
                lo, nkb = kwin(i)
                for kb in range(nkb + 2):
                    vsrc = V[:, lo + kb, h * 64:(h + 1) * 64] if kb < nkb else vc[:, kb - nkb, h * 64:(h + 1) * 64]
                    fns.append(mm(ps[:, bo, i * 64:(i + 1) * 64], PT[:, sb2, kb * 128:(kb + 1) * 128], vsrc, kb == 0, kb == nkb + 1))
                P.pe(fns, reads=(('PT', sb2), 'vc') + tuple(('V', lo + kb) for kb in range(nkb)), writes=(('ps', bo),))
                if i == 7:
                    sso = 32 + (h % 2) * 32
                    P.op('dve', lambda e: e.reciprocal(out=st[:, sso + 24:sso + 32], in_=st[:, sso:sso + 8]),
                         reads=(('ssum', h % 2),), writes=(('srin', h % 2),))
                    P.op('dve', tt(PQO[:, :, h * 64:(h + 1) * 64], ps[:, bo, :].rearrange("p (i d) -> p i d", i=8),
                                   st[:, sso + 24:sso + 32].unsqueeze(2).broadcast_to([128, 8, 64]), ALU.mult),
                         reads=(('ps', bo), ('srin', h % 2)), writes=tuple(('pqo', ii) for ii in range(8)))

            NIT = NH * 8
            t_load(0)
            att_scores(0)
            att_scores(1)
            att_max(0)
            att_max(1)
            att_exp(0)
            for n in range(NIT):
                if n % 8 == 0 and n // 8 + 1 < NH:
                    t_load(n // 8 + 1)
                if n + 2 < NIT:
                    att_scores(n + 2)
                    att_max(n + 2)
                if n + 1 < NIT:
                    att_exp(n + 1)
                att_tr(n)
                if n >= 1:
                    att_pv(n - 1)
            att_pv(NIT - 1)
            P.banks = BODY
            for i in range(8):
                ith = i // 4
                b = P.psum()
                fns = [tr(psb16(b)[:, c * 128:(c + 1) * 128], PQO[:, i, c * 128:(c + 1) * 128], ident_b[:]) for c in range(4)]
                P.pe(fns, reads=(('pqo', i), 'identb'), writes=(('ps', b),))
                P.op('act' if i % 2 else 'dve',
                     (act(actT[:, 12:16, i * 128:(i + 1) * 128], psb16(b)[:, 0:512].rearrange("p (c q) -> p c q", c=4), AF.Identity)
                      if i % 2 else cp(actT[:, 12:16, i * 128:(i + 1) * 128], psb16(b)[:, 0:512].rearrange("p (c q) -> p c q", c=4))),
                     reads=(('ps', b),), writes=tuple(kA(12 + c, ith) for c in range(4)))
            if DBG and l == 0:
                t = P.dma('pool', lambda e: e.dma_start(out=dbg_d.rearrange("p (c t) -> p c t", c=8), in_=actT[:, 12:20, :]), 'o_dbg',
                          reads=tuple(kA(12 + c, th) for c in range(8) for th in range(2)))
                P.out_tickets[t[0]] = t
            out_proj(lambda l=l: wo_d[l], 8, lambda fc, th: oTt(fc)[:, th * 512:(th + 1) * 512], lambda fc, th: kA(12 + fc, th),
                     after_th0=cb[0], late_th1=cb[1])

        for tb in range(8):
            i = tb % 4
            P.dma('sp', lambda e, tb=tb, i=i: e.dma_start(out=xstage(i), in_=x_d[tb * 128:(tb + 1) * 128, :]),
                  'c_x%d' % i, writes=kXS(i))
            for half in range(2):
                b = P.psum()
                fns = [tr(ps[:, b, c * 128:(c + 1) * 128], xstage(i)[:, (half * 4 + c) * 128:(half * 4 + c + 1) * 128], ident_f[:])
                       for c in range(4)]
                P.pe(fns, reads=kXS(i) + ('identf',), writes=(('ps', b),))
                th = tb // 4
                for c in range(4):
                    dc = half * 4 + c
                    P.op('dve' if half else 'act',
                         (cp(xT[:, dc, tb * 128:(tb + 1) * 128], ps[:, b, c * 128:(c + 1) * 128]) if half else
                          act(xT[:, dc, tb * 128:(tb + 1) * 128], ps[:, b, c * 128:(c + 1) * 128], AF.Identity)),
                         reads=(('ps', b),), writes=(kX(dc, th),))

        def seq(k):
            return divmod(k, 3)

        NSUB = NL * 3
        P.banks = list(range(8))
        mbs = {0: 0}
        MG['q'].append(mod_gen(0, 0))
        for _ in range(4):
            mod_step()
        if NSUB > 1:
            mbs[1] = 1
            MG['q'].append(mod_gen(*seq(1)))
        MG['busy'] = True
        P.banks = BND
        norm_modulate(mbs[0], 0)
        norm_modulate(mbs[0], 1, 'sq')
        for k in range(NSUB):
            l, s_ = seq(k)
            P.banks = BODY_B
            if k + 2 < NSUB:
                l2, s2 = seq(k + 2)
                mbs[k + 2] = (l2 * 3 + s2) % 3
                MG['q'].append(mod_gen(l2, s2))

            def mid(k=k):
                norm_modulate(mbs[k], 1, 'rest')
                P.banks = BODY

            def after_th0(k=k):
                P.banks = BND
                post_stats(0)
                post_update(mbs[k], 0)
                if k + 1 < NSUB:
                    norm_modulate(mbs[k + 1], 0, 'sq')
                P.banks = BODY_A

            def late_th1(k=k):
                if k + 1 < NSUB:
                    norm_modulate(mbs[k + 1], 0, 'rest')
            cb = (after_th0, late_th1)
            if s_ == 0:
                layer_loads(l)
                ffn(l, 0, mid, cb)
            elif s_ == 1:
                mixer(l, mid, cb)
            else:
                ffn(l, 1, mid, cb)
            mod_drain()
            P.banks = BND
            post_stats(1)
            post_update(mbs[k], 1)
            if k + 1 < NSUB:
                norm_modulate(mbs[k + 1], 1, 'sq')
        P.banks = list(range(8))

        for tb in range(8):
            i = tb % 4
            th = tb // 4
            for half in range(2):
                b = P.psum()
                fns = [tr(ps[:, b, c * 128:(c + 1) * 128], xT[:, half * 4 + c, tb * 128:(tb + 1) * 128], ident_f[:]) for c in range(4)]
                P.pe(fns, reads=tuple(kX(half * 4 + c, th) for c in range(4)) + ('identf',), writes=(('ps', b),))
                if half == 0:
                    P.op('act', act(xstage(i)[:, 0:512], psb(b), AF.Identity), reads=(('ps', b),), writes=kXS(i)[0:2])
                else:
                    P.op('dve', cp(xstage(i)[:, 512:1024], psb(b)), reads=(('ps', b),), writes=kXS(i)[2:4])
            t = P.dma('sp', lambda e, tb=tb, i=i: e.dma_start(out=y_d[tb * 128:(tb + 1) * 128, :], in_=xstage(i)),
                      'o_y%d' % i, reads=kXS(i))
            P.out_tickets[t[0]] = t
        P.wait_all('sp', list(P.out_tickets.values()))

    P.reset(True)
    P.tiles = []
    emit_all2()
    tiles = P.tiles
    P.reset(False)
    P.tiles = tiles
    emit_all2()

    semnames = list(ENGS) + sorted(P.dcnt.keys())
    semh = {}
    for sname in semnames:
        semh[sname] = es.enter_context(nc.semaphore("s_" + sname))

    def replay(eng, e):
        for waits, fn, incs, incv in P.ops[eng]:
            for s, v in waits:
                e.wait_ge(semh[s], v)
            if fn is None:
                continue
            ins = fn(e)
            if incs is not None:
                ins.then_inc(semh[incs], incv)

    with nc.Block() as block:
        @block.tensor
        def _(e):
            replay('pe', e)

        @block.scalar
        def _(e):
            replay('act', e)

        @block.vector
        def _(e):
            replay('dve', e)

        @block.gpsimd
        def _(e):
            replay('pool', e)

        @block.sync
        def _(e):
            replay('sp', e)
    es.close()
    return nc


NEG = -1e30


def _masks(sample):
    OH = np.zeros((128, 1024), np.float32)
    W = np.zeros((128, 1280), np.float32)
    t = np.arange(1024)
    for r in range(16):
        oh = (t // 64 == r).astype(np.float32)
        if sample:
            rs = min(max(r - 4, 0), 8)
            krow = t // 64
            w = np.where((krow >= rs) & (krow < rs + 8), 0.0, NEG).astype(np.float32)
            wc = np.zeros(256, np.float32)
        else:
            w = np.where(t // 256 == (r * 64) // 256, 0.0, NEG).astype(np.float32)
            wc = np.full(256, NEG, np.float32)
        for base in (0,):
            OH[base + r] = oh
            W[base + r, :1024] = w
            W[base + r, 1024:] = wc
    return np.concatenate([OH, W], axis=1)


def _tbias(rpb):
    L = rpb.shape[0]
    a = np.arange(2)[:, None, None, None, None]
    qc = np.arange(64)[None, :, None, None, None]
    dl = np.arange(-4, 5)[None, None, :, None, None]
    ka = np.arange(2)[None, None, None, :, None]
    kc = np.arange(64)[None, None, None, None, :]
    dr = 2 * dl + ka - a
    ro = dr + 7
    valid = (ro >= 0) & (ro <= 14)
    co = np.clip(kc - qc + 15, 0, 30)
    cs = np.clip(qc - 8, 0, 48)
    cmask = (kc >= cs) & (kc < cs + 16)
    roc = np.clip(ro, 0, 14)
    shp = (2, 64, 9, 2, 64)
    roc_b = np.broadcast_to(roc, shp)
    co_b = np.broadcast_to(co, shp)
    g = rpb[:, :, roc_b, co_b]
    g = np.where(np.broadcast_to(valid, shp)[None, None], g, np.float32(0.0))
    g = np.where(np.broadcast_to(cmask, shp)[None, None], g, np.float32(NEG))
    return np.ascontiguousarray(g.reshape(L, NH, 128, 9 * 128).astype(np.float32))


def _dft(sample):
    n = 1024 if sample else 256
    k = np.arange(n)
    ang = 2.0 * np.pi * ((k[:, None] * k[None, :]) % n) / n
    c = (np.cos(ang) / np.sqrt(n)).astype(np.float32)
    s = (-np.sin(ang) / np.sqrt(n)).astype(np.float32)
    if sample:
        return np.stack([c, s])
    out = np.zeros((2, 1024, 1024), np.float32)
    for b in range(4):
        out[0, b * 256:(b + 1) * 256, b * 256:(b + 1) * 256] = c
        out[1, b * 256:(b + 1) * 256, b * 256:(b + 1) * 256] = s
    return out


def _cc():
    k = np.arange(64)
    ang = 2.0 * np.pi * ((k[:, None] * k[None, :]) % 64) / 64
    c = (np.cos(ang) / 8.0).astype(np.float32)
    s = (np.sin(ang) / 8.0).astype(np.float32)
    out = np.zeros((128, 256), np.float32)
    for g in range(2):
        out[g * 64:(g + 1) * 64, g * 64:(g + 1) * 64] = c
        out[g * 64:(g + 1) * 64, 128 + g * 64:128 + (g + 1) * 64] = s
    return out


def _fm(v):
    return np.ascontiguousarray(v.reshape(-1, 128).T)


_NC_CACHE = {}


def make_in_maps(x_prompt, x_sample, cache_k, cache_v, c, c_ctx, ada_w, ada_b, norm_pre, norm_post,
                 ffn_w_in, ffn_w_out, w_in, w_out, rpb, gmlp_norm, gmlp_w, gmlp_b, _NL=None):
    f32 = np.float32
    NL = int(_NL) if _NL is not None else int(ada_w.shape[0])
    A = lambda a: np.ascontiguousarray(np.asarray(a, dtype=f32))
    x_prompt, x_sample, cache_k, cache_v, c, c_ctx = map(A, (x_prompt, x_sample, cache_k, cache_v, c, c_ctx))
    ada_w, ada_b, norm_pre, norm_post = A(ada_w)[:NL], A(ada_b)[:NL], A(norm_pre)[:NL], A(norm_post)[:NL]
    ffn_w_in, ffn_w_out, w_in, w_out = A(ffn_w_in)[:NL], A(ffn_w_out)[:NL], A(w_in)[:NL], A(w_out)[:NL]
    rpb, gmlp_norm, gmlp_w, gmlp_b = A(rpb)[:NL], A(gmlp_norm)[:NL], A(gmlp_w)[:NL], A(gmlp_b)[:NL]

    def vecs_for(cv):
        cols = []
        for l in range(NL):
            cols.append(_fm(ada_b[l]))
            cols.append(_fm(norm_pre[l].reshape(-1)))
            cols.append(_fm(norm_post[l].reshape(-1)))
        cols.append(_fm(cv))
        gn = np.zeros((128, 8), f32)
        for l in range(NL):
            gn[:, l * 2:(l + 1) * 2] = _fm(gmlp_norm[l].reshape(-1))
        cols.append(gn)
        return np.ascontiguousarray(np.concatenate(cols, axis=1))

    gwT = np.ascontiguousarray(np.transpose(gmlp_w, (0, 3, 1, 2)).reshape(NL, 128, 4 * 128))
    gbb = np.ascontiguousarray(
        np.broadcast_to(gmlp_b.reshape(NL, 2, 2, 1, 128), (NL, 2, 2, 64, 128)).transpose(0, 2, 3, 1, 4).reshape(NL, 128, 2 * 128))
    ident = np.eye(128, dtype=f32)
    cc = _cc()
    tb_s = _tbias(rpb)
    tb_p = np.zeros_like(tb_s)
    mk_s, mk_p = _masks(True), _masks(False)
    dft_s, dft_p = _dft(True), _dft(False)
    zkv = np.zeros((NL, 256, 512), f32)
    shared = dict(ada_w=ada_w, ffn_w_in=ffn_w_in, ffn_w_out=ffn_w_out, w_in=w_in, w_out=w_out,
                  cc=cc, gwT=gwT, gbb=gbb, ident=ident)
    vec_p = vecs_for(c_ctx)
    in_maps = []
    for core in range(8):
        m = dict(shared)
        if core < 4:
            m.update(x=np.ascontiguousarray(x_prompt[4 * core:4 * core + 4].reshape(T, D)), vecs=vec_p,
                     ck=zkv, cv=zkv, tb=tb_p, mk=mk_p, dft=dft_p)
        else:
            b = core - 4
            m.update(x=np.ascontiguousarray(x_sample[b]), vecs=vecs_for(c[b]),
                     ck=np.ascontiguousarray(cache_k[b, :NL].reshape(NL, 256, 512)),
                     cv=np.ascontiguousarray(cache_v[b, :NL].reshape(NL, 256, 512)),
                     tb=tb_s, mk=mk_s, dft=dft_s)
        in_maps.append(m)
    return NL, in_maps


def kernel(**inputs):
    f32 = np.float32
    NL, in_maps = make_in_maps(**inputs)
    if NL not in _NC_CACHE:
        _NC_CACHE[NL] = build(NL)
    nc = _NC_CACHE[NL]
    res = run_bass_kernel_spmd(nc, in_maps, core_ids=list(range(8)))
    R = res.results
    y_prompt = np.concatenate([R[i]["y"].reshape(4, 256, D) for i in range(4)], axis=0)
    y_sample = np.stack([R[4 + b]["y"] for b in range(4)], axis=0)
    nk = np.concatenate([R[i]["nk"].reshape(NL, 4, 256, NH, 64).transpose(1, 0, 2, 3, 4) for i in range(4)], axis=0)
    nv = np.concatenate([R[i]["nv"].reshape(NL, 4, 256, NH, 64).transpose(1, 0, 2, 3, 4) for i in range(4)], axis=0)
    return (y_prompt.astype(f32), y_sample.astype(f32), np.ascontiguousarray(nk.astype(f32)), np.ascontiguousarray(nv.astype(f32)))
```

```python
import numpy as np
from contextlib import ExitStack
import concourse.bass as bass
import concourse.mybir as mybir
from concourse.bass_utils import run_bass_kernel_spmd

F32 = mybir.dt.float32
BF16 = mybir.dt.bfloat16
AF = mybir.ActivationFunctionType
ALU = mybir.AluOpType
AX = mybir.AxisListType

D = 1024
T = 1024
DFF = 2816
NFC = 22
NH = 8
NSLOT = 5
TH_OUTER_FFN = False
SLOT = 4096
KW = 896
EPS = 1e-6
NVL = 120
ENGS = ('pe', 'act', 'dve', 'pool', 'sp')


class Prog:
    def __init__(self):
        self.reset(True)
        self.tiles = []

    def reset(self, dry):
        self.dry = dry
        self.ops = {e: [] for e in ENGS}
        self.cnt = {e: 0 for e in ENGS}
        self.seen = {e: {} for e in ENGS}
        self.bw = {}
        self.br = {}
        self.dcnt = {}
        self.ntile = 0
        self.nloaded = 0
        self.psn = 0
        self.tmpn = 0
        self.out_tickets = {}
        self.nreleased = 0
        self.banks = list(range(8))

    def _deps(self, eng, reads, writes):
        deps = {}

        def add(t):
            if t is None:
                return
            s, v = t
            if deps.get(s, 0) < v:
                deps[s] = v
        for k in reads:
            add(self.bw.get(k))
        for k in writes:
            add(self.bw.get(k))
            for s, v in self.br.get(k, {}).items():
                add((s, v))
        waits = []
        for s, v in deps.items():
            if s == 'pe' and eng == 'pe':
                continue
            if self.seen[eng].get(s, 0) < v:
                self.seen[eng][s] = v
                waits.append((s, v))
        return waits

    def _commit(self, ticket, reads, writes):
        for k in writes:
            self.bw[k] = ticket
            self.br[k] = {}
        s, v = ticket
        for k in reads:
            d = self.br.setdefault(k, {})
            if d.get(s, 0) < v:
                d[s] = v

    def op(self, eng, fn, reads=(), writes=()):
        waits = self._deps(eng, reads, writes)
        self.cnt[eng] += 1
        ticket = (eng, self.cnt[eng])
        self.ops[eng].append((waits, fn, eng, 1))
        self._commit(ticket, reads, writes)
        return ticket

    def pe(self, fns, reads=(), writes=(), fine=None):
        waits = self._deps('pe', reads, writes)
        self.cnt['pe'] += 1
        ticket = ('pe', self.cnt['pe'])
        n = len(fns)
        allreads = tuple(reads)
        for i, fn in enumerate(fns):
            w = waits if i == 0 else []
            if fine is not None:
                w = w + self._deps('pe', fine[i], ())
                allreads = allreads + tuple(fine[i])
            self.ops['pe'].append((w, fn, 'pe' if i == n - 1 else None, 1))
        self._commit(ticket, allreads, writes)
        return ticket

    def dma(self, q, fn, dsem, reads=(), writes=()):
        waits = self._deps(q, reads, writes)
        self.dcnt[dsem] = self.dcnt.get(dsem, 0) + 16
        ticket = (dsem, self.dcnt[dsem])
        self.ops[q].append((waits, fn, dsem, 16))
        self._commit(ticket, reads, writes)
        return ticket

    def wait_all(self, eng, tickets):
        waits = []
        for s, v in tickets:
            if self.seen[eng].get(s, 0) < v:
                self.seen[eng][s] = v
                waits.append((s, v))
        self.ops[eng].append((waits, None, None, 0))

    def psum(self, n=1):
        b = self.banks[self.psn % len(self.banks)]
        self.psn += 1
        return b

    def tmp(self):
        i = self.tmpn % 3
        self.tmpn += 1
        return i


def build(NL=4, DBG=False):
    nc = bass.Bass("TRN2", target_bir_lowering=False)

    def din(name, shape):
        return nc.dram_tensor(name, shape, F32, kind="ExternalInput").ap()

    def dout(name, shape):
        return nc.dram_tensor(name, shape, F32, kind="ExternalOutput").ap()

    x_d = din("x", [T, D])
    vecs_d = din("vecs", [128, NL * NVL + 16])
    adaw_d = din("ada_w", [NL, D, 9 * D])
    fwi_d = din("ffn_w_in", [NL, 2, D, 2 * DFF])
    fwo_d = din("ffn_w_out", [NL, 2, DFF, D])
    wi_d = din("w_in", [NL, D, 2304])
    wo_d = din("w_out", [NL, D, D])
    ck_d = din("ck", [NL, 256, 512])
    cv_d = din("cv", [NL, 256, 512])
    tb_d = din("tb", [NL, NH, 128, 9 * 128])
    mk_d = din("mk", [128, 1024 + 1280])
    dft_d = din("dft", [2, T, T])
    cc_d = din("cc", [128, 256])
    gw_d = din("gwT", [NL, 128, 4 * 128])
    gb_d = din("gbb", [NL, 128, 2 * 128])
    id_d = din("ident", [128, 128])
    y_d = dout("y", [T, D])
    nk_d = dout("nk", [NL, T, 512])
    nv_d = dout("nv", [NL, T, 512])
    dbg_d = dout("dbg", [128, 8 * T]) if DBG else None

    P = Prog()
    es = ExitStack()

    def sb(name, shape, dt):
        return es.enter_context(nc.sbuf_tensor("sb_" + name, shape, dt))

    xT = sb("xT", [128, 8, T], F32)
    Y = sb("Y", [128, 8, 2, 512], F32)
    actT = sb("actT", [128, NFC, T], BF16)
    ring = sb("ring", [128, NSLOT, SLOT], BF16)
    tmpf = sb("tmpf", [128, 3, 512], F32)
    stage = sb("stage", [128, 2, 512], F32)
    Tt = sb("Tt", [128, 2, 9 * 128], BF16)
    qz = sb("qz", [128, 4, 128], BF16)
    Pb = sb("Pb", [128, 2, KW], BF16)
    PT = sb("PT", [128, 2, KW], BF16)
    V = sb("V", [128, 8, 512], BF16)
    PQO = sb("PQO", [128, 8, 512], BF16)
    kctok = sb("kctok", [128, 2, 512], BF16)
    vc = sb("vc", [128, 2, 512], BF16)
    kcT = sb("kcT", [128, 4, 256], BF16)
    mk = sb("mk", [128, 1024 + 1280], BF16)
    ccs = sb("ccs", [128, 256], BF16)
    gwT = sb("gwT_s", [128, 4, 128], BF16)
    gbb = sb("gbb_s", [128, 2, 128], F32)
    vecs = sb("vecs_s", [128, NL * NVL + 16], F32)
    MOD = sb("MOD", [128, 3, 72], F32)
    ABG = sb("ABG", [128, 3, 24], F32)
    scb = sb("scb", [128, 8], BF16)
    ident_f = sb("ident_f", [128, 128], F32)
    ident_b = sb("ident_b", [128, 128], BF16)
    ones_b = sb("ones_b", [128, 128], BF16)
    epsc = sb("epsc", [128, 1], F32)
    st = sb("st", [128, 160], F32)
    gvt = sb("gvt", [128, 2, 256], F32)
    ps = es.enter_context(nc.psum_tensor("psum_all", [128, 8, 512], F32))

    def hT(dc, th):
        return Y[:, dc, th, 0:256].bitcast(BF16)

    def sqv(dc):
        c, th = dc // 2, dc % 2
        return actT[:, c, th * 512:(th + 1) * 512]

    def kY(dc, th):
        return ('Y', dc, th)

    def kA(c, th):
        return ('act', c, th)

    def kX(dc, th):
        return ('x', dc, th)

    def psb(b):
        return ps[:, b, :]

    def psb16(b):
        return ps[:, b, :].bitcast(BF16)

    def emit_load(n):
        spec = P.tiles[n]
        slot = n % NSLOT
        kind = spec[0]
        sview = ring[:, slot, :]
        if kind == 'A8':
            _, parts, Wt = spec
            for pi, (src, c0, w, d0) in enumerate(parts):
                dst = sview[:, 0:8 * Wt].rearrange("p (c w) -> p c w", c=8)[:, :, d0:d0 + w]
                srcap = src()[:, c0:c0 + w].rearrange("(c p) w -> p c w", p=128)
                tk = P.dma('pool', (lambda e, dst=dst, srcap=srcap: e.dma_start(out=dst, in_=srcap)),
                           'ring%d' % slot, reads=(), writes=(('ring', slot, pi),))
            for pi in range(2):
                P.bw[('ring', slot, pi)] = tk
                P.br.setdefault(('ring', slot, pi), {})
        elif kind == 'WO':
            _, src, nf, d0 = spec
            dst = sview[:, 0:nf * 128].rearrange("p (c w) -> p c w", c=nf)
            srcap = src()[:, d0:d0 + 128].rearrange("(c p) w -> p c w", p=128)
            P.dma('pool', (lambda e, dst=dst, srcap=srcap: e.dma_start(out=dst, in_=srcap)),
                  'ring%d' % slot, reads=(), writes=(('ring', slot, 0), ('ring', slot, 1)))

    def _fill():
        while P.nloaded < min(len(P.tiles), P.nreleased + NSLOT):
            emit_load(P.nloaded)
            P.nloaded += 1

    def acquire(spec):
        n = P.ntile
        P.ntile += 1
        if P.dry:
            P.tiles.append(spec)
        else:
            _fill()
            assert n < P.nloaded
        return n % NSLOT

    def release():
        P.nreleased += 1
        if not P.dry:
            _fill()

    def kR(slot):
        return (('ring', slot, 0), ('ring', slot, 1))

    def ring8(slot, W=512):
        return ring[:, slot, 0:8 * W].rearrange("p (c w) -> p c w", c=8)

    def ringwo(slot, nf):
        return ring[:, slot, 0:nf * 128].rearrange("p (c w) -> p c w", c=nf)

    def mm(out, lhsT, rhs, start, stop):
        return lambda e: e.matmul(out, lhsT=lhsT, rhs=rhs, start=start, stop=stop)

    def tr(out, in_, ident):
        return lambda e: e.transpose(out, in_, ident)

    def act(out, in_, func, bias=None, scale=None, accum=None):
        kw = {}
        if bias is not None:
            kw['bias'] = bias
        if scale is not None:
            kw['scale'] = scale
        if accum is not None:
            kw['accum_out'] = accum
        return lambda e: e.activation(out=out, in_=in_, func=func, **kw)

    def tt(out, in0, in1, op):
        return lambda e: e.tensor_tensor(out=out, in0=in0, in1=in1, op=op)

    def ts(out, in0, s1, op0, s2=None, op1=None):
        if op1 is None:
            return lambda e: e.tensor_scalar(out=out, in0=in0, scalar1=s1, scalar2=None, op0=op0)
        return lambda e: e.tensor_scalar(out=out, in0=in0, scalar1=s1, scalar2=s2, op0=op0, op1=op1)

    def stt(out, in0, scalar, in1, op0, op1):
        return lambda e: e.scalar_tensor_tensor(out=out, in0=in0, scalar=scalar, in1=in1, op0=op0, op1=op1)

    def cp(out, in_):
        return lambda e: e.tensor_copy(out=out, in_=in_)

    def xstage(i):
        return actT[:, 2 * i:2 * i + 2, :].rearrange("p c t -> p (c t)").bitcast(F32)

    def kXS(i):
        return (kA(2 * i, 0), kA(2 * i, 1), kA(2 * i + 1, 0), kA(2 * i + 1, 1))

    def emit_all2():
        P.nreleased = 0
        cvo = NL * NVL
        P.dma('sp', lambda e: e.dma_start(out=vecs[:], in_=vecs_d), 'c_vecs', writes=('vecs',))
        P.dma('sp', lambda e: e.dma_start(out=ident_f[:], in_=id_d), 'c_id', writes=('identf',))
        P.dma('pool', lambda e: e.dma_start(out=mk[:, 0:1024], in_=mk_d[:, 0:1024]), 'c_mk', writes=('mk',))
        P.dma('pool', lambda e: e.dma_start(out=mk[:, 1024:2304], in_=mk_d[:, 1024:2304]), 'c_mk2', writes=('mk2',))
        P.dma('pool', lambda e: e.dma_start(out=ccs[:], in_=cc_d), 'c_cc', writes=('ccs',))
        P.op('dve', cp(ident_b[:], ident_f[:]), reads=('identf',), writes=('identb',))
        P.op('dve', lambda e: e.memset(ones_b[:], 1.0), writes=('ones',))
        P.op('dve', lambda e: e.memset(epsc[:], EPS), writes=('eps',))
        P.op('dve', lambda e: e.memset(qz[:], 0.0), writes=tuple(('qz', j) for j in range(4)))
        P.op('act', act(scb[:], vecs[:, cvo:cvo + 8], AF.Silu), reads=('vecs',), writes=('scb',))

        BODY = [0, 1, 2, 4, 5, 6, 7]
        BND = [0, 1]
        BODY_A = [2, 3, 6, 7]
        BODY_B = [2, 4, 6, 7]

        def mod_gen(l, s):
            mb = (l * 3 + s) % 3
            b = 3
            for t6 in range(6):
                c0 = s * 3072 + t6 * 512
                slot = acquire(('A8', [(lambda l=l: adaw_d[l], c0, 512, 0)], 512))
                w8 = ring8(slot)
                fns = []
                for c4 in range(4):
                    col = t6 * 4 + c4
                    for dc in range(8):
                        fns.append(mm(ps[:, b, col:col + 1], w8[:, dc, c4 * 128:(c4 + 1) * 128],
                                      scb[:, dc:dc + 1], dc == 0, dc == 7))
                P.pe(fns, reads=kR(slot) + ('scb',), writes=(('ps', b),))
                release()
                vo = l * NVL + s * 24
                m0 = s * 24
                if t6 == 3:
                    npre = l * NVL + 72 + s * 8
                    P.op('dve', tt(MOD[:, mb, m0:m0 + 16], ps[:, b, 0:16], vecs[:, vo:vo + 16], ALU.add),
                         reads=(('ps', b), 'vecs'), writes=(('mod', mb),))
                    P.op('dve', stt(ABG[:, mb, 0:8], MOD[:, mb, m0 + 8:m0 + 16], 1.0, vecs[:, npre:npre + 8], ALU.add, ALU.mult),
                         reads=(('mod', mb), 'vecs'), writes=(('abgA', mb),))
                    P.op('dve', cp(ABG[:, mb, 8:16], MOD[:, mb, m0:m0 + 8]), reads=(('mod', mb),), writes=(('abgB', mb),))
                if t6 == 5:
                    npost = l * NVL + 96 + s * 8
                    coef = 1.0 if s == 1 else 0.5
                    P.op('dve', tt(MOD[:, mb, m0 + 16:m0 + 24], ps[:, b, 16:24], vecs[:, vo + 16:vo + 24], ALU.add),
                         reads=(('ps', b), 'vecs'), writes=(('modg', mb),))
                    P.op('dve', stt(ABG[:, mb, 16:24], MOD[:, mb, m0 + 16:m0 + 24], coef, vecs[:, npost:npost + 8], ALU.mult, ALU.mult),
                         reads=(('modg', mb), 'vecs'), writes=(('abgG', mb),))
                yield

        MG = {'q': [], 'busy': False}

        def mod_step():
            while MG['q']:
                try:
                    next(MG['q'][0])
                    break
                except StopIteration:
                    MG['q'].pop(0)
            MG['busy'] = len(MG['q']) > 1

        def mod_drain():
            while MG['q']:
                mod_step()

        def emit_mod(l, s):
            MG['q'].append(mod_gen(l, s))
            mod_drain()
            return (l * 3 + s) % 3

        def sqY(dc, th):
            return V[:, dc, :] if th == 0 else PQO[:, dc, :]

        def kSQ(dc, th):
            return ('V', dc) if th == 0 else ('pqo', dc)

        def stats_to_rstd(sq_fn, sq_keys, rb):
            b = P.psum()
            fns = [mm(psb(b), ones_b[:], sq_fn(dc), dc == 0, dc == 7) for dc in range(8)]
            P.pe(fns, reads=('ones',), writes=(('ps', b),), fine=[(sq_keys(dc),) for dc in range(8)])
            t1 = P.tmp()
            P.op('act', act(tmpf[:, t1, :], psb(b), AF.Ln, bias=epsc[:], scale=1.0 / D),
                 reads=(('ps', b), 'eps'), writes=(('tmp', t1),))
            P.op('act', act(psb(rb), tmpf[:, t1, :], AF.Exp, scale=-0.5),
                 reads=(('tmp', t1),), writes=(('ps', rb),))

        def norm_modulate(mb, th, phase='all'):
            rb = 4 + th
            sqf = (lambda dc: V[:, dc, :]) if th == 0 else (lambda dc: PQO[:, dc, :])
            sqk = (lambda dc: ('V', dc)) if th == 0 else (lambda dc: ('pqo', dc))
            if phase in ('all', 'sq'):
                for dc in range(8):
                    P.op('act', act(sqf(dc), xT[:, dc, th * 512:(th + 1) * 512], AF.Square), reads=(kX(dc, th),), writes=(sqk(dc),))
            if phase == 'sq':
                return
            stats_to_rstd(sqf, sqk, rb)
            for dc in range(8):
                t2 = P.tmp()
                P.op('dve', tt(tmpf[:, t2, :], xT[:, dc, th * 512:(th + 1) * 512], psb(rb), ALU.mult),
                     reads=(kX(dc, th), ('ps', rb)), writes=(('tmp', t2),))
                P.op('act', act(hT(dc, th), tmpf[:, t2, :], AF.Identity, bias=ABG[:, mb, 8 + dc:9 + dc],
                                scale=ABG[:, mb, dc:dc + 1]),
                     reads=(('tmp', t2), ('abgA', mb), ('abgB', mb)), writes=(kY(dc, th),))

        def post_stats(th):
            stats_to_rstd(lambda dc: sqY(dc, th), lambda dc: kSQ(dc, th), 4 + th)

        def post_update(mb, th, dcs=range(8)):
            rb = 4 + th
            dcs = list(dcs)
            for p0 in range(0, len(dcs), 2):
                pair = dcs[p0:p0 + 2]
                banks = {}
                for dc in pair:
                    tbk = P.psum()
                    banks[dc] = tbk
                    P.op('dve', tt(psb(tbk), Y[:, dc, th, :], psb(rb), ALU.mult),
                         reads=(kY(dc, th), ('ps', rb)), writes=(('ps', tbk),))
                for dc in pair:
                    tbk = banks[dc]
                    xs = xT[:, dc, th * 512:(th + 1) * 512]
                    P.op('dve', stt(xs, psb(tbk), ABG[:, mb, 16 + dc:17 + dc], xs, ALU.mult, ALU.add),
                         reads=(('ps', tbk), ('abgG', mb), kX(dc, th)), writes=(kX(dc, th),))

        def out_proj(src_fn, nf, in_view, in_keys, hook_every=0, after_th0=None, late_th1=None, th_outer=True):
            if not th_outer:
                LAG = 3
                slots = {}

                def group(dc, th):
                    w = ringwo(slots[dc], nf)
                    b = P.psum()
                    fns = [mm(psb(b), w[:, fc, :], in_view(fc, th), fc == 0, fc == nf - 1) for fc in range(nf)]
                    P.pe(fns, reads=kR(slots[dc]), writes=(('ps', b),), fine=[(in_keys(fc, th),) for fc in range(nf)])
                    P.op('dve', cp(Y[:, dc, th, :], psb(b)), reads=(('ps', b),), writes=(kY(dc, th),))
                    P.op('act', act(sqY(dc, th), Y[:, dc, th, :], AF.Square), reads=(kY(dc, th),), writes=(kSQ(dc, th),))
                for t in range(8 + LAG):
                    if t < 8:
                        slots[t] = acquire(('WO', src_fn, nf, t * 128))
                        group(t, 0)
                        if t == 7 and after_th0 is not None:
                            after_th0()
                    if t - LAG >= 0:
                        group(t - LAG, 1)
                        release()
                        if t - LAG == 6 and late_th1 is not None:
                            late_th1()
                return
            for th in range(2):
                for dc in range(8):
                    slot = acquire(('WO', src_fn, nf, dc * 128))
                    w = ringwo(slot, nf)
                    b = P.psum()
                    fns = [mm(psb(b), w[:, fc, :], in_view(fc, th), fc == 0, fc == nf - 1) for fc in range(nf)]
                    P.pe(fns, reads=kR(slot) + tuple(in_keys(fc, th) for fc in range(nf)), writes=(('ps', b),))
                    P.op('dve', cp(Y[:, dc, th, :], psb(b)), reads=(('ps', b),), writes=(kY(dc, th),))
                    P.op('act', act(sqY(dc, th), Y[:, dc, th, :], AF.Square), reads=(kY(dc, th),), writes=(kSQ(dc, th),))
                    release()
                    if th == 0 and hook_every and ((dc % hook_every) == hook_every - 1 or MG['busy']):
                        mod_step()
                    if th == 1 and dc == 3 and late_th1 is not None:
                        late_th1()
                if th == 0:
                    mod_drain()
                    if after_th0 is not None:
                        after_th0()

        def layer_loads(l):
            P.dma('pool', lambda e: e.dma_start(out=kctok[:], in_=ck_d[l].rearrange("(b p) f -> p b f", p=128)),
                  'c_kc', writes=('kctok',))
            P.dma('pool', lambda e: e.dma_start(out=vc[:], in_=cv_d[l].rearrange("(b p) f -> p b f", p=128)),
                  'c_vc', writes=('vc',))
            P.dma('pool', lambda e: e.dma_start(out=gwT[:], in_=gw_d[l].rearrange("p (g q) -> p g q", g=4)),
                  'c_gw', writes=('gwT',))
            P.dma('sp', lambda e: e.dma_start(out=gbb[:], in_=gb_d[l].rearrange("p (g q) -> p g q", g=2)),
                  'c_gb', writes=('gbb',))

        def kct_transposes():
            b = P.psum()
            fns = []
            for kb in range(2):
                for c in range(4):
                    fns.append(tr(psb16(b)[:, (c * 2 + kb) * 128:(c * 2 + kb + 1) * 128],
                                  kctok[:, kb, c * 128:(c + 1) * 128], ident_b[:]))
            P.pe(fns, reads=('kctok', 'identb'), writes=(('ps', b),))
            P.op('dve', cp(kcT[:].rearrange("p c k -> p (c k)"), psb16(b)), reads=(('ps', b),), writes=('kcT',))

        def ffn(l, i, mid, cb):
            def grp(slot, g, jj, th):
                w8 = ring8(slot)
                j = 2 * g + jj
                bg = P.psum()
                bu = P.psum()
                rk = kR(slot)
                fine = [(kY(dc, th),) for dc in range(8)]
                P.pe([mm(psb(bg), w8[:, dc, jj * 128:(jj + 1) * 128], hT(dc, th), dc == 0, dc == 7)
                      for dc in range(8)], reads=rk, writes=(('ps', bg),), fine=fine)
                P.pe([mm(psb(bu), w8[:, dc, 256 + jj * 128:256 + (jj + 1) * 128], hT(dc, th), dc == 0, dc == 7)
                      for dc in range(8)], reads=rk, writes=(('ps', bu),), fine=fine)
                t1 = P.tmp()
                P.op('act', act(tmpf[:, t1, :], psb(bg), AF.Silu), reads=(('ps', bg),), writes=(('tmp', t1),))
                P.op('dve', tt(actT[:, j, th * 512:(th + 1) * 512], tmpf[:, t1, :], psb(bu), ALU.mult),
                     reads=(('tmp', t1), ('ps', bu)), writes=(kA(j, th),))

            def wspec(g):
                return ('A8', [(lambda l=l, i=i: fwi_d[l, i], 256 * g, 256, 0),
                               (lambda l=l, i=i: fwi_d[l, i], DFF + 256 * g, 256, 256)], 512)
            s0 = acquire(wspec(0))
            s1 = acquire(wspec(1))
            s2 = acquire(wspec(2))
            for th in range(2):
                for g, slot in ((0, s0), (1, s1), (2, s2)):
                    for jj in range(2):
                        grp(slot, g, jj, th)
                    if th == 0 and g == 1:
                        mid()
            release()
            release()
            release()
            for g in range(3, 11):
                slot = acquire(wspec(g))
                for jj in range(2):
                    for th in range(2):
                        grp(slot, g, jj, th)
                release()
                if TH_OUTER_FFN:
                    if g % 3 == 1 or MG['busy']:
                        mod_step()
                else:
                    mod_step()
                    if MG['busy']:
                        mod_step()
                if g == 6 and i == 0:
                    kct_transposes()
            if not TH_OUTER_FFN:
                mod_drain()
            out_proj(lambda l=l, i=i: fwo_d[l, i], NFC,
                     lambda fc, th: actT[:, fc, th * 512:(th + 1) * 512], lambda fc, th: kA(fc, th), hook_every=2,
                     after_th0=cb[0], late_th1=cb[1], th_outer=TH_OUTER_FFN)

        def qT(c): return actT[:, c, :]
        def kTt(c): return actT[:, 4 + c, :]
        def fTt(c): return actT[:, 8 + c, :]
        def uTt(c): return actT[:, 10 + c, :]
        def oTt(c): return actT[:, 12 + c, :]
        vn = actT[:, 20:22, :].rearrange("p c t -> p (c t)").rearrange("p (b w) -> p b w", b=8)

        def kVN(tb):
            return kA(20 + tb // 4, (tb % 4) // 2)

        def mixer(l, mid, cb):
            hk = lambda th: tuple(kY(dc, th) for dc in range(8))
            slotA = acquire(('A8', [(lambda l=l: wi_d[l], 0, 512, 0)], 512))
            slotD = acquire(('A8', [(lambda l=l: wi_d[l], 1536, 512, 0)], 512))
            wA, wD = ring8(slotA), ring8(slotD)

            def grpA(c, th):
                b = P.psum()
                P.pe([mm(psb(b), wA[:, dc, c * 128:(c + 1) * 128], hT(dc, th), dc == 0, dc == 7) for dc in range(8)],
                     reads=kR(slotA), writes=(('ps', b),), fine=[(kY(dc, th),) for dc in range(8)])
                P.op('act', act(qT(c)[:, th * 512:(th + 1) * 512], psb(b), AF.Identity, scale=0.125),
                     reads=(('ps', b),), writes=(kA(c, th),))

            def grpD(c, th):
                b = P.psum()
                P.pe([mm(psb(b), wD[:, dc, c * 128:(c + 1) * 128], hT(dc, th), dc == 0, dc == 7) for dc in range(8)],
                     reads=kR(slotD), writes=(('ps', b),), fine=[(kY(dc, th),) for dc in range(8)])
                if c < 2:
                    P.op('dve', cp(fTt(c)[:, th * 512:(th + 1) * 512], psb(b)), reads=(('ps', b),), writes=(kA(8 + c, th),))
                else:
                    P.op('act', act(uTt(c - 2)[:, th * 512:(th + 1) * 512], psb(b), AF.Gelu_apprx_tanh),
                         reads=(('ps', b),), writes=(kA(10 + c - 2, th),))
            for c in range(4):
                grpA(c, 0)
            mid()
            for c in range(4):
                grpD(c, 0)
            for c in range(4):
                grpA(c, 1)
            for c in range(4):
                grpD(c, 1)
            release()
            release()
            mod_step()
            mod_step()
            for tb in range(8):
                th, tc = tb // 4, (tb % 4) * 128
                b = P.psum()
                fns = [mm(ps[:, b, k2 * 256:(k2 + 1) * 256], fTt(k2)[:, tb * 128:(tb + 1) * 128], ccs[:], True, True) for k2 in range(2)]
                P.pe(fns, reads=(kA(8, th), kA(9, th), 'ccs'), writes=(('ps', b),))
                P.op('act', act(PQO[:, tb, :], psb(b), AF.Identity), reads=(('ps', b),), writes=(('pqo', tb),))

            slot = acquire(('A8', [(lambda l=l: wi_d[l], 2048, 256, 0)], 256))
            w8 = ring8(slot, 256)
            gno = NL * NVL + 8
            for tb in range(8):
                th, tc = tb // 4, (tb % 4) * 128
                b = P.psum()
                P.pe([mm(ps[:, b, 0:256], hT(dc, th)[:, tc:tc + 128], w8[:, dc, :], dc == 0, dc == 7) for dc in range(8)],
                     reads=kR(slot) + hk(th), writes=(('ps', b),))
                g2 = tb % 2
                P.op('act', act(gvt[:, g2, :], ps[:, b, 0:256], AF.Gelu_apprx_tanh), reads=(('ps', b),), writes=(('gvt', g2),))
                so = g2 * 8
                P.op('dve', lambda e, g2=g2, so=so: e.tensor_reduce(out=st[:, so:so + 4], in_=gvt[:, g2, :].rearrange("p (g c) -> p g c", g=4),
                                                                     axis=AX.X, op=ALU.add),
                     reads=(('gvt', g2),), writes=(('st', g2),))
                P.op('dve', ts(st[:, so + 4:so + 8], st[:, so:so + 4], -1.0 / 64, ALU.mult), reads=(('st', g2),), writes=(('st', g2),))
                P.op('dve', tt(Y[:, tb, 0, 256:512].rearrange("p (g c) -> p g c", g=4), gvt[:, g2, :].rearrange("p (g c) -> p g c", g=4),
                               st[:, so + 4:so + 8].unsqueeze(2).broadcast_to([128, 4, 64]), ALU.add),
                     reads=(('gvt', g2), ('st', g2)), writes=(('cen', tb),))
                P.op('dve', tt(gvt[:, g2, :], Y[:, tb, 0, 256:512], Y[:, tb, 0, 256:512], ALU.mult), reads=(('cen', tb),), writes=(('gvt', g2),))
                P.op('dve', lambda e, g2=g2, tb=tb: e.tensor_reduce(out=st[:, 128 + tb * 4:132 + tb * 4], in_=gvt[:, g2, :].rearrange("p (g c) -> p g c", g=4),
                                                                     axis=AX.X, op=ALU.add),
                     reads=(('gvt', g2),), writes=(('lnss',),))
            release()
            mod_step()

            slot = acquire(('A8', [(lambda l=l: wi_d[l], 512, 512, 0)], 512))
            w8 = ring8(slot)
            for c in range(4):
                for th in range(2):
                    b = P.psum()
                    P.pe([mm(psb(b), w8[:, dc, c * 128:(c + 1) * 128], hT(dc, th), dc == 0, dc == 7) for dc in range(8)],
                         reads=kR(slot) + hk(th), writes=(('ps', b),))
                    P.op('dve', cp(kTt(c)[:, th * 512:(th + 1) * 512], psb(b)), reads=(('ps', b),), writes=(kA(4 + c, th),))
            for tb in range(8):
                th, tc = tb // 4, (tb % 4) * 128
                b = P.psum()
                P.pe([mm(psb(b), hT(dc, th)[:, tc:tc + 128], w8[:, dc, :], dc == 0, dc == 7) for dc in range(8)],
                     reads=kR(slot) + hk(th), writes=(('ps', b),))
                sg = tb % 2
                P.op('act', act(stage[:, sg, :], psb(b), AF.Identity), reads=(('ps', b),), writes=(('stage', sg),))
                t = P.dma('sp', lambda e, tb=tb, sg=sg: e.dma_start(out=nk_d[l, tb * 128:(tb + 1) * 128, :], in_=stage[:, sg, :]),
                          'o_st%d' % sg, reads=(('stage', sg),))
                P.out_tickets[t[0]] = t
            release()
            mod_step()
            s_c = acquire(('A8', [(lambda: dft_d[0], 0 * 512, 512, 0)], 512))
            s_s = acquire(('A8', [(lambda: dft_d[1], 0 * 512, 512, 0)], 512))
            wc, wsn = ring8(s_c), ring8(s_s)
            for k2 in range(2):
                b = P.psum()
                fns = []
                for nb in range(8):
                    fns.append(mm(psb(b), PQO[:, nb, k2 * 256:k2 * 256 + 128], wc[:, nb, :], nb == 0, False))
                    fns.append(mm(psb(b), PQO[:, nb, k2 * 256 + 128:k2 * 256 + 256], wsn[:, nb, :], False, nb == 7))
                P.pe(fns, reads=kR(s_c) + kR(s_s) + tuple(('pqo', nb) for nb in range(8)), writes=(('ps', b),))
                P.op('act', act(oTt(4 + k2)[:, 0 * 512:(0 + 1) * 512], psb(b), AF.Identity), reads=(('ps', b),), writes=(kA(16 + k2, 0),))
            release()
            release()
            mod_step()

            P.op('act', act(st[:, 128:160], st[:, 128:160], AF.Sqrt, bias=epsc[:], scale=1.0 / 64),
                 reads=(('lnss',), 'eps'), writes=(('lnss',),))
            P.op('dve', lambda e: e.reciprocal(out=st[:, 128:160], in_=st[:, 128:160]), reads=(('lnss',),), writes=(('lnss',),))
            for tb in range(8):
                P.op('dve', tt(vn[:, tb, :].rearrange("p (g c) -> p g c", g=4), Y[:, tb, 0, 256:512].rearrange("p (g c) -> p g c", g=4),
                               st[:, 128 + tb * 4:132 + tb * 4].unsqueeze(2).broadcast_to([128, 4, 64]), ALU.mult),
                     reads=(('cen', tb), ('lnss',)), writes=(kVN(tb),))

            slot = acquire(('A8', [(lambda l=l: wi_d[l], 1024, 512, 0)], 512))
            w8 = ring8(slot)
            for tb in range(8):
                th, tc = tb // 4, (tb % 4) * 128
                b = P.psum()
                P.pe([mm(psb(b), hT(dc, th)[:, tc:tc + 128], w8[:, dc, :], dc == 0, dc == 7) for dc in range(8)],
                     reads=kR(slot) + hk(th), writes=(('ps', b),))
                sg = tb % 2
                P.op('act', act(stage[:, sg, :], psb(b), AF.Identity), reads=(('ps', b),), writes=(('stage', sg),))
                P.op('dve', cp(V[:, tb, :], stage[:, sg, :]), reads=(('stage', sg),), writes=(('V', tb),))
                t = P.dma('sp', lambda e, tb=tb, sg=sg: e.dma_start(out=nv_d[l, tb * 128:(tb + 1) * 128, :], in_=stage[:, sg, :]),
                          'o_st%d' % sg, reads=(('stage', sg),))
                P.out_tickets[t[0]] = t
            release()
            mod_step()
            s_c = acquire(('A8', [(lambda: dft_d[0], 1 * 512, 512, 0)], 512))
            s_s = acquire(('A8', [(lambda: dft_d[1], 1 * 512, 512, 0)], 512))
            wc, wsn = ring8(s_c), ring8(s_s)
            for k2 in range(2):
                b = P.psum()
                fns = []
                for nb in range(8):
                    fns.append(mm(psb(b), PQO[:, nb, k2 * 256:k2 * 256 + 128], wc[:, nb, :], nb == 0, False))
                    fns.append(mm(psb(b), PQO[:, nb, k2 * 256 + 128:k2 * 256 + 256], wsn[:, nb, :], False, nb == 7))
                P.pe(fns, reads=kR(s_c) + kR(s_s) + tuple(('pqo', nb) for nb in range(8)), writes=(('ps', b),))
                P.op('act', act(oTt(4 + k2)[:, 1 * 512:(1 + 1) * 512], psb(b), AF.Identity), reads=(('ps', b),), writes=(kA(16 + k2, 1),))
            release()
            release()
            mod_step()

            for tb in range(8):
                th, tc = tb // 4, (tb % 4) * 128
                b = P.psum()
                fns = []
                for k2 in range(2):
                    for gg in range(2):
                        g = 2 * k2 + gg
                        fns.append(mm(ps[gg * 64:(gg + 1) * 64, b, k2 * 128:(k2 + 1) * 128], vn[:, tb, g * 64:(g + 1) * 64],
                                      gwT[:, g, :], True, True))
                P.pe(fns, reads=(kVN(tb), 'gwT'), writes=(('ps', b),))
                for k2 in range(2):
                    t1 = P.tmp()
                    P.op('dve', stt(tmpf[:, t1, 0:128], ps[:, b, k2 * 128:(k2 + 1) * 128], vecs[:, gno + l * 2 + k2:gno + l * 2 + k2 + 1],
                                    gbb[:, k2, :], ALU.mult, ALU.add),
                         reads=(('ps', b), 'vecs', 'gbb'), writes=(('tmp', t1),))
                    P.op('dve', tt(oTt(6 + k2)[:, tb * 128:(tb + 1) * 128], tmpf[:, t1, 0:128], uTt(k2)[:, tb * 128:(tb + 1) * 128], ALU.mult),
                         reads=(('tmp', t1), kA(10 + k2, th)), writes=(kA(18 + k2, th),))

            OH = mk[:, 0:1024]
            Wm = mk[:, 1024:2304]
            mod_step()
            mod_drain()

            def t_load(h):
                tbuf = h % 2
                P.dma('pool', lambda e, h=h, tbuf=tbuf: e.dma_start(out=Tt[:, tbuf, :], in_=tb_d[l, h]), 'c_T%d' % tbuf, writes=(('Tt', tbuf),))

            def geom(n):
                h, i = divmod(n, 8)
                lo = min(max(i - 2, 0), 3)
                return h, i, h // 2, (h % 2) * 64, lo, 2 * (n % 3), n % 2

            def kwin(i):
                if i <= 1:
                    return 0, 4
                if i >= 6:
                    return 4, 4
                return min(max(i - 2, 0), 3), 5

            def att_scores(n):
                h, i, hc, pb, lo, bs, sb2 = geom(n)
                tbuf = h % 2
                qi = (h % 2) * 2 + (i % 2)
                P.op('pool', cp(qz[pb:pb + 64, qi, :], actT[pb:pb + 64, hc, i * 128:(i + 1) * 128]),
                     reads=(kA(hc, i // 4),), writes=(('qz', qi),))
                qsl = qz[:, qi, :]
                ohs = OH[:, i * 128:(i + 1) * 128]
                lo, nkb = kwin(i)
                k0 = lo * 128
                d0 = (lo - i + 4) * 128
                kch = actT[:, 4 + hc, :]
                Sps = ps[:, bs:bs + 2, :].rearrange("p b w -> p (b w)")
                if nkb == 5:
                    fns = [
                        mm(Sps[:, 0:512], qsl, kch[:, k0:k0 + 512], True, False),
                        mm(Sps[:, 0:512], ohs, Wm[:, k0:k0 + 512], False, False),
                        mm(Sps[:, 0:512], ident_b[:], Tt[:, tbuf, d0:d0 + 512], False, True),
                        mm(Sps[:, 512:640], qsl, kch[:, k0 + 512:k0 + 640], True, False),
                        mm(Sps[:, 640:896], qsl, kcT[:, hc, :], False, False),
                        mm(Sps[:, 512:640], ohs, Wm[:, k0 + 512:k0 + 640], False, False),
                        mm(Sps[:, 640:896], ohs, Wm[:, 1024:1280], False, False),
                        mm(Sps[:, 512:640], ident_b[:], Tt[:, tbuf, d0 + 512:d0 + 640], False, True),
                    ]
                else:
                    fns = [
                        mm(Sps[:, 0:512], qsl, kch[:, k0:k0 + 512], True, False),
                        mm(Sps[:, 0:512], ohs, Wm[:, k0:k0 + 512], False, False),
                        mm(Sps[:, 0:512], ident_b[:], Tt[:, tbuf, d0:d0 + 512], False, True),
                        mm(Sps[:, 512:768], qsl, kcT[:, hc, :], True, False),
                        mm(Sps[:, 512:768], ohs, Wm[:, 1024:1280], False, True),
                    ]
                P.pe(fns, reads=(('qz', qi), kA(4 + hc, 0), kA(4 + hc, 1), 'kcT', 'mk', 'mk2', ('Tt', tbuf), 'identb'),
                     writes=(('ps', bs), ('ps', bs + 1)))

            def att_max(n):
                h, i, hc, pb, lo, bs, sb2 = geom(n)
                Sps = ps[:, bs:bs + 2, :].rearrange("p b w -> p (b w)")
                mo = 96 + (n % 3) * 4
                Wd = kwin(i)[1] * 128 + 256
                P.op('dve', lambda e: e.tensor_reduce(out=st[:, mo:mo + 1], in_=Sps[:, 0:Wd], axis=AX.X, op=ALU.max, negate=True),
                     reads=(('ps', bs), ('ps', bs + 1)), writes=(('stm', n % 3),))

            def att_exp(n):
                h, i, hc, pb, lo, bs, sb2 = geom(n)
                sso = 32 + (h % 2) * 32
                Sps = ps[:, bs:bs + 2, :].rearrange("p b w -> p (b w)")
                mo = 96 + (n % 3) * 4
                Wd = kwin(i)[1] * 128 + 256
                P.op('act', act(Pb[:, sb2, 0:Wd], Sps[:, 0:Wd], AF.Exp, bias=st[:, mo:mo + 1], scale=1.0,
                                accum=st[:, sso + i:sso + i + 1]),
                     reads=(('ps', bs), ('ps', bs + 1), ('stm', n % 3)), writes=(('Pb', sb2), ('ssum', h % 2)))

            def att_tr(n):
                h, i, hc, pb, lo, bs, sb2 = geom(n)
                bt = 6
                nt = kwin(i)[1] + 2
                Wd = nt * 128
                fns = [tr(psb16(bt)[:, kb * 128:(kb + 1) * 128], Pb[:, sb2, kb * 128:(kb + 1) * 128], ident_b[:]) for kb in range(nt)]
                P.pe(fns, reads=(('Pb', sb2), 'identb'), writes=(('ps', bt),))
                if n % 2 == 0:
                    P.op('act', act(PT[:, sb2, 0:Wd], psb16(bt)[:, 0:Wd], AF.Identity), reads=(('ps', bt),), writes=(('PT', sb2),))
                else:
                    P.op('dve', cp(PT[:, sb2, 0:Wd], psb16(bt)[:, 0:Wd]), reads=(('ps', bt),), writes=(('PT', sb2),))

            def att_pv(n):
                h, i, hc, pb, lo, bs, sb2 = geom(n)
                bo = 7
                fns = []
                lo, nkb = kwin(i)
                for kb in range(nkb + 2):
                    vsrc = V[:, lo + kb, h * 64:(h + 1) * 64] if kb < nkb else vc[:, kb - nkb, h * 64:(h + 1) * 64]
                    fns.append(mm(ps[:, bo, i * 64:(i + 1) * 64], PT[:, sb2, kb * 128:(kb + 1) * 128], vsrc, kb == 0, kb == nkb + 1))
                P.pe(fns, reads=(('PT', sb2), 'vc') + tuple(('V', lo + kb) for kb in range(nkb)), writes=(('ps', bo),))
                if i == 7:
                    sso = 32 + (h % 2) * 32
                    P.op('dve', lambda e: e.reciprocal(out=st[:, sso + 24:sso + 32], in_=st[:, sso:sso + 8]),
                         reads=(('ssum', h % 2),), writes=(('srin', h % 2),))
                    P.op('dve', tt(PQO[:, :, h * 64:(h + 1) * 64], ps[:, bo, :].rearrange("p (i d) -> p i d", i=8),
                                   st[:, sso + 24:sso + 32].unsqueeze(2).broadcast_to([128, 8, 64]), ALU.mult),
                         reads=(('ps', bo), ('srin', h % 2)), writes=tuple(('pqo', ii) for ii in range(8)))

            NIT = NH * 8
            t_load(0)
            att_scores(0)
            att_scores(1)
            att_max(0)
            att_max(1)
            att_exp(0)
            for n in range(NIT):
                if n % 8 == 0 and n // 8 + 1 < NH:
                    t_load(n // 8 + 1)
                if n + 2 < NIT:
                    att_scores(n + 2)
                    att_max(n + 2)
                if n + 1 < NIT:
                    att_exp(n + 1)
                att_tr(n)
                if n >= 1:
                    att_pv(n - 1)
            att_pv(NIT - 1)
            P.banks = BODY
            for i in range(8):
                ith = i // 4
                b = P.psum()
                fns = [tr(psb16(b)[:, c * 128:(c + 1) * 128], PQO[:, i, c * 128:(c + 1) * 128], ident_b[:]) for c in range(4)]
                P.pe(fns, reads=(('pqo', i), 'identb'), writes=(('ps', b),))
                P.op('act' if i % 2 else 'dve',
                     (act(actT[:, 12:16, i * 128:(i + 1) * 128], psb16(b)[:, 0:512].rearrange("p (c q) -> p c q", c=4), AF.Identity)
                      if i % 2 else cp(actT[:, 12:16, i * 128:(i + 1) * 128], psb16(b)[:, 0:512].rearrange("p (c q) -> p c q", c=4))),
                     reads=(('ps', b),), writes=tuple(kA(12 + c, ith) for c in range(4)))
            if DBG and l == 0:
                t = P.dma('pool', lambda e: e.dma_start(out=dbg_d.rearrange("p (c t) -> p c t", c=8), in_=actT[:, 12:20, :]), 'o_dbg',
                          reads=tuple(kA(12 + c, th) for c in range(8) for th in range(2)))
                P.out_tickets[t[0]] = t
            out_proj(lambda l=l: wo_d[l], 8, lambda fc, th: oTt(fc)[:, th * 512:(th + 1) * 512], lambda fc, th: kA(12 + fc, th),
                     after_th0=cb[0], late_th1=cb[1])

        for tb in range(8):
            i = tb % 4
            P.dma('sp', lambda e, tb=tb, i=i: e.dma_start(out=xstage(i), in_=x_d[tb * 128:(tb + 1) * 128, :]),
                  'c_x%d' % i, writes=kXS(i))
            for half in range(2):
                b = P.psum()
                fns = [tr(ps[:, b, c * 128:(c + 1) * 128], xstage(i)[:, (half * 4 + c) * 128:(half * 4 + c + 1) * 128], ident_f[:])
                       for c in range(4)]
                P.pe(fns, reads=kXS(i) + ('identf',), writes=(('ps', b),))
                th = tb // 4
                for c in range(4):
                    dc = half * 4 + c
                    P.op('dve' if half else 'act',
                         (cp(xT[:, dc, tb * 128:(tb + 1) * 128], ps[:, b, c * 128:(c + 1) * 128]) if half else
                          act(xT[:, dc, tb * 128:(tb + 1) * 128], ps[:, b, c * 128:(c + 1) * 128], AF.Identity)),
                         reads=(('ps', b),), writes=(kX(dc, th),))

        def seq(k):
            return divmod(k, 3)

        NSUB = NL * 3
        P.banks = list(range(8))
        mbs = {0: 0}
        MG['q'].append(mod_gen(0, 0))
        for _ in range(4):
            mod_step()
        if NSUB > 1:
            mbs[1] = 1
            MG['q'].append(mod_gen(*seq(1)))
        MG['busy'] = True
        P.banks = BND
        norm_modulate(mbs[0], 0)
        norm_modulate(mbs[0], 1, 'sq')
        for k in range(NSUB):
            l, s_ = seq(k)
            P.banks = BODY_B
            if k + 2 < NSUB:
                l2, s2 = seq(k + 2)
                mbs[k + 2] = (l2 * 3 + s2) % 3
                MG['q'].append(mod_gen(l2, s2))

            def mid(k=k):
                norm_modulate(mbs[k], 1, 'rest')
                P.banks = BODY

            def after_th0(k=k):
                P.banks = BND
                post_stats(0)
                post_update(mbs[k], 0)
                if k + 1 < NSUB:
                    norm_modulate(mbs[k + 1], 0, 'sq')
                P.banks = BODY_A

            def late_th1(k=k):
                if k + 1 < NSUB:
                    norm_modulate(mbs[k + 1], 0, 'rest')
            cb = (after_th0, late_th1)
            if s_ == 0:
                layer_loads(l)
                ffn(l, 0, mid, cb)
            elif s_ == 1:
                mixer(l, mid, cb)
            else:
                ffn(l, 1, mid, cb)
            mod_drain()
            P.banks = BND
            post_stats(1)
            post_update(mbs[k], 1)
            if k + 1 < NSUB:
                norm_modulate(mbs[k + 1], 1, 'sq')
        P.banks = list(range(8))

        for tb in range(8):
            i = tb % 4
            th = tb // 4
            for half in range(2):
                b = P.psum()
                fns = [tr(ps[:, b, c * 128:(c + 1) * 128], xT[:, half * 4 + c, tb * 128:(tb + 1) * 128], ident_f[:]) for c in range(4)]
                P.pe(fns, reads=tuple(kX(half * 4 + c, th) for c in range(4)) + ('identf',), writes=(('ps', b),))
                if half == 0:
                    P.op('act', act(xstage(i)[:, 0:512], psb(b), AF.Identity), reads=(('ps', b),), writes=kXS(i)[0:2])
                else:
                    P.op('dve', cp(xstage(i)[:, 512:1024], psb(b)), reads=(('ps', b),), writes=kXS(i)[2:4])
            t = P.dma('sp', lambda e, tb=tb, i=i: e.dma_start(out=y_d[tb * 128:(tb + 1) * 128, :], in_=xstage(i)),
                      'o_y%d' % i, reads=kXS(i))
            P.out_tickets[t[0]] = t
        P.wait_all('sp', list(P.out_tickets.values()))

    P.reset(True)
    P.tiles = []
    emit_all2()
    tiles = P.tiles
    P.reset(False)
    P.tiles = tiles
    emit_all2()

    semnames = list(ENGS) + sorted(P.dcnt.keys())
    semh = {}
    for sname in semnames:
        semh[sname] = es.enter_context(nc.semaphore("s_" + sname))

    def replay(eng, e):
        for waits, fn, incs, incv in P.ops[eng]:
            for s, v in waits:
                e.wait_ge(semh[s], v)
            if fn is None:
                continue
            ins = fn(e)
            if incs is not None:
                ins.then_inc(semh[incs], incv)

    with nc.Block() as block:
        @block.tensor
        def _(e):
            replay('pe', e)

        @block.scalar
        def _(e):
            replay('act', e)

        @block.vector
        def _(e):
            replay('dve', e)

        @block.gpsimd
        def _(e):
            replay('pool', e)

        @block.sync
        def _(e):
            replay('sp', e)
    es.close()
    return nc


NEG = -1e30


def _masks(sample):
    OH = np.zeros((128, 1024), np.float32)
    W = np.zeros((128, 1280), np.float32)
    t = np.arange(1024)
    for r in range(16):
        oh = (t // 64 == r).astype(np.float32)
        if sample:
            rs = min(max(r - 4, 0), 8)
            krow = t // 64
            w = np.where((krow >= rs) & (krow < rs + 8), 0.0, NEG).astype(np.float32)
            wc = np.zeros(256, np.float32)
        else:
            w = np.where(t // 256 == (r * 64) // 256, 0.0, NEG).astype(np.float32)
            wc = np.full(256, NEG, np.float32)
        for base in (0,):
            OH[base + r] = oh
            W[base + r, :1024] = w
            W[base + r, 1024:] = wc
    return np.concatenate([OH, W], axis=1)


def _tbias(rpb):
    L = rpb.shape[0]
    a = np.arange(2)[:, None, None, None, None]
    qc = np.arange(64)[None, :, None, None, None]
    dl = np.arange(-4, 5)[None, None, :, None, None]
    ka = np.arange(2)[None, None, None, :, None]
    kc = np.arange(64)[None, None, None, None, :]
    dr = 2 * dl + ka - a
    ro = dr + 7
    valid = (ro >= 0) & (ro <= 14)
    co = np.clip(kc - qc + 15, 0, 30)
    cs = np.clip(qc - 8, 0, 48)
    cmask = (kc >= cs) & (kc < cs + 16)
    roc = np.clip(ro, 0, 14)
    shp = (2, 64, 9, 2, 64)
    roc_b = np.broadcast_to(roc, shp)
    co_b = np.broadcast_to(co, shp)
    g = rpb[:, :, roc_b, co_b]
    g = np.where(np.broadcast_to(valid, shp)[None, None], g, np.float32(0.0))
    g = np.where(np.broadcast_to(cmask, shp)[None, None], g, np.float32(NEG))
    return np.ascontiguousarray(g.reshape(L, NH, 128, 9 * 128).astype(np.float32))


def _dft(sample):
    n = 1024 if sample else 256
    k = np.arange(n)
    ang = 2.0 * np.pi * ((k[:, None] * k[None, :]) % n) / n
    c = (np.cos(ang) / np.sqrt(n)).astype(np.float32)
    s = (-np.sin(ang) / np.sqrt(n)).astype(np.float32)
    if sample:
        return np.stack([c, s])
    out = np.zeros((2, 1024, 1024), np.float32)
    for b in range(4):
        out[0, b * 256:(b + 1) * 256, b * 256:(b + 1) * 256] = c
        out[1, b * 256:(b + 1) * 256, b * 256:(b + 1) * 256] = s
    return out


def _cc():
    k = np.arange(64)
    ang = 2.0 * np.pi * ((k[:, None] * k[None, :]) % 64) / 64
    c = (np.cos(ang) / 8.0).astype(np.float32)
    s = (np.sin(ang) / 8.0).astype(np.float32)
    out = np.zeros((128, 256), np.float32)
    for g in range(2):
        out[g * 64:(g + 1) * 64, g * 64:(g + 1) * 64] = c
        out[g * 64:(g + 1) * 64, 128 + g * 64:128 + (g + 1) * 64] = s
    return out


def _fm(v):
    return np.ascontiguousarray(v.reshape(-1, 128).T)


_NC_CACHE = {}


def make_in_maps(x_prompt, x_sample, cache_k, cache_v, c, c_ctx, ada_w, ada_b, norm_pre, norm_post,
                 ffn_w_in, ffn_w_out, w_in, w_out, rpb, gmlp_norm, gmlp_w, gmlp_b, _NL=None):
    f32 = np.float32
    NL = int(_NL) if _NL is not None else int(ada_w.shape[0])
    A = lambda a: np.ascontiguousarray(np.asarray(a, dtype=f32))
    x_prompt, x_sample, cache_k, cache_v, c, c_ctx = map(A, (x_prompt, x_sample, cache_k, cache_v, c, c_ctx))
    ada_w, ada_b, norm_pre, norm_post = A(ada_w)[:NL], A(ada_b)[:NL], A(norm_pre)[:NL], A(norm_post)[:NL]
    ffn_w_in, ffn_w_out, w_in, w_out = A(ffn_w_in)[:NL], A(ffn_w_out)[:NL], A(w_in)[:NL], A(w_out)[:NL]
    rpb, gmlp_norm, gmlp_w, gmlp_b = A(rpb)[:NL], A(gmlp_norm)[:NL], A(gmlp_w)[:NL], A(gmlp_b)[:NL]

    def vecs_for(cv):
        cols = []
        for l in range(NL):
            cols.append(_fm(ada_b[l]))
            cols.append(_fm(norm_pre[l].reshape(-1)))
            cols.append(_fm(norm_post[l].reshape(-1)))
        cols.append(_fm(cv))
        gn = np.zeros((128, 8), f32)
        for l in range(NL):
            gn[:, l * 2:(l + 1) * 2] = _fm(gmlp_norm[l].reshape(-1))
        cols.append(gn)
        return np.ascontiguousarray(np.concatenate(cols, axis=1))

    gwT = np.ascontiguousarray(np.transpose(gmlp_w, (0, 3, 1, 2)).reshape(NL, 128, 4 * 128))
    gbb = np.ascontiguousarray(
        np.broadcast_to(gmlp_b.reshape(NL, 2, 2, 1, 128), (NL, 2, 2, 64, 128)).transpose(0, 2, 3, 1, 4).reshape(NL, 128, 2 * 128))
    ident = np.eye(128, dtype=f32)
    cc = _cc()
    tb_s = _tbias(rpb)
    tb_p = np.zeros_like(tb_s)
    mk_s, mk_p = _masks(True), _masks(False)
    dft_s, dft_p = _dft(True), _dft(False)
    zkv = np.zeros((NL, 256, 512), f32)
    shared = dict(ada_w=ada_w, ffn_w_in=ffn_w_in, ffn_w_out=ffn_w_out, w_in=w_in, w_out=w_out,
                  cc=cc, gwT=gwT, gbb=gbb, ident=ident)
    vec_p = vecs_for(c_ctx)
    in_maps = []
    for core in range(8):
        m = dict(shared)
        if core < 4:
            m.update(x=np.ascontiguousarray(x_prompt[4 * core:4 * core + 4].reshape(T, D)), vecs=vec_p,
                     ck=zkv, cv=zkv, tb=tb_p, mk=mk_p, dft=dft_p)
        else:
            b = core - 4
            m.update(x=np.ascontiguousarray(x_sample[b]), vecs=vecs_for(c[b]),
                     ck=np.ascontiguousarray(cache_k[b, :NL].reshape(NL, 256, 512)),
                     cv=np.ascontiguousarray(cache_v[b, :NL].reshape(NL, 256, 512)),
                     tb=tb_s, mk=mk_s, dft=dft_s)
        in_maps.append(m)
    return NL, in_maps


def kernel(**inputs):
    f32 = np.float32
    NL, in_maps = make_in_maps(**inputs)
    if NL not in _NC_CACHE:
        _NC_CACHE[NL] = build(NL)
    nc = _NC_CACHE[NL]
    res = run_bass_kernel_spmd(nc, in_maps, core_ids=list(range(8)))
    R = res.results
    y_prompt = np.concatenate([R[i]["y"].reshape(4, 256, D) for i in range(4)], axis=0)
    y_sample = np.stack([R[4 + b]["y"] for b in range(4)], axis=0)
    nk = np.concatenate([R[i]["nk"].reshape(NL, 4, 256, NH, 64).transpose(1, 0, 2, 3, 4) for i in range(4)], axis=0)
    nv = np.concatenate([R[i]["nv"].reshape(NL, 4, 256, NH, 64).transpose(1, 0, 2, 3, 4) for i in range(4)], axis=0)
    return (y_prompt.astype(f32), y_sample.astype(f32), np.ascontiguousarray(nk.astype(f32)), np.ascontiguousarray(nv.astype(f32)))
```
